# Optimizing a Trainium2 kernel written in Bass

```python
import math
import jax, jax.numpy as jnp
from jax import lax
import numpy as np

D_MODEL = 4096
BATCH = 2
SEQ = 8192
DEPTH = 2

BRANCH_W = D_MODEL // 2
N_BRANCH = 3
RET_HEAD_DIM = 256
RET_W = BRANCH_W
RET_HEADS = RET_W // RET_HEAD_DIM
RET_CHUNK = 128
DSA_HEAD_DIM = 128
DSA_W = BRANCH_W
DSA_HEADS = DSA_W // DSA_HEAD_DIM
DSA_KV_HEADS = 4
DSA_KV_W = DSA_KV_HEADS * DSA_HEAD_DIM
IDX_HEADS = 32
IDX_DIM = 128
TOPK_MAX = 256
Q_BLOCK = 128
GM_W = BRANCH_W
GM_GROUPS = 16
GM_GROUP_DIM = GM_W // GM_GROUPS
GM_CHUNK = 128

ROPE_THETA = 10000.0
EPS = 1e-6

IN_SPLITS = (RET_W, RET_W, RET_W, RET_W,
             DSA_W, DSA_KV_W, DSA_KV_W,
             IDX_HEADS * IDX_DIM, IDX_DIM, IDX_HEADS,
             DSA_W,
             GM_W, GM_W, GM_W,
             N_BRANCH * D_MODEL)
N_IN = sum(IN_SPLITS)

kernel_name = "hybrid_retention_dsa_gmlp_gated"


def rms_norm(x, g):
    xf = x.astype(jnp.float32)
    y = xf * lax.rsqrt(jnp.mean(xf * xf, axis=-1, keepdims=True) + EPS)
    return (y * g.astype(jnp.float32)).astype(x.dtype)


def layer_norm(x, g):
    xf = x.astype(jnp.float32)
    mu = jnp.mean(xf, axis=-1, keepdims=True)
    xc = xf - mu
    y = xc * lax.rsqrt(jnp.mean(xc * xc, axis=-1, keepdims=True) + EPS)
    return (y * g.astype(jnp.float32)).astype(x.dtype)


def rope(x, pos):
    d = x.shape[-1]
    half = d // 2
    inv = 1.0 / (ROPE_THETA ** (jnp.arange(half, dtype=jnp.float32) * 2.0 / d))
    ang = pos.astype(jnp.float32)[..., None] * inv
    cos = jnp.cos(ang)[:, :, None, :]
    sin = jnp.sin(ang)[:, :, None, :]
    x1 = x[..., :half].astype(jnp.float32)
    x2 = x[..., half:].astype(jnp.float32)
    return jnp.concatenate([x1 * cos - x2 * sin, x2 * cos + x1 * sin], axis=-1).astype(x.dtype)


def retention(q, k, v, pos):
    B, S, H, dk = q.shape
    dv = v.shape[-1]
    C = RET_CHUNK
    N = S // C
    q = rope(q, pos).astype(jnp.float32)
    k = rope(k, pos).astype(jnp.float32) * (dk ** -0.5)
    v = v.astype(jnp.float32)
    log_g = jnp.log(1.0 - jnp.power(2.0, -5.0 - jnp.arange(H, dtype=jnp.float32)))
    n = jnp.arange(C, dtype=jnp.float32)
    diff = n[:, None] - n[None, :]
    decay = jnp.where(diff >= 0, jnp.exp(log_g[:, None, None] * jnp.maximum(diff, 0.0)), 0.0)
    xi = jnp.exp(log_g[:, None] * (n + 1.0))
    zeta = jnp.exp(log_g[:, None] * (C - 1.0 - n))
    chunk_decay = jnp.exp(log_g * C)

    def to_chunks(a):
        return a.reshape(B, N, C, H, a.shape[-1]).transpose(1, 0, 3, 2, 4)

    def step(R, qkv):
        qc, kc, vc = qkv
        inner = jnp.einsum('bhnm,bhme->bhne', jnp.einsum('bhnd,bhmd->bhnm', qc, kc) * decay, vc)
        cross = jnp.einsum('bhnd,bhde->bhne', qc * xi[..., None], R)
        R = R * chunk_decay[:, None, None] + jnp.einsum('bhmd,bhme->bhde', kc * zeta[..., None], vc)
        return R, inner + cross

    R0 = jnp.zeros((B, H, dk, dv), jnp.float32)
    _, out = lax.scan(step, R0, (to_chunks(q), to_chunks(k), to_chunks(v)))
    return out.transpose(1, 0, 3, 2, 4).reshape(B, S, H, dv)


def dsa_attention(q, k, v, q_idx, k_idx, w_idx, topk):
    B, S, H, dh = q.shape
    KVH = k.shape[2]
    G = H // KVH
    nb = S // Q_BLOCK
    key_pos = jnp.arange(S)

    def block(args):
        qb, qib, wb, t0 = args
        t_pos = t0 + jnp.arange(Q_BLOCK)
        s = jnp.einsum('bthd,bsd->bths', qib, k_idx).astype(jnp.float32) * (IDX_DIM ** -0.5)
        w = wb.astype(jnp.float32) * (IDX_HEADS ** -0.5)
        score = jnp.einsum('bth,bths->bts', w, jax.nn.relu(s))
        causal = key_pos[None, :] <= t_pos[:, None]
        score = jnp.where(causal[None], score, -jnp.inf)
        _, sel = lax.top_k(score, topk)
        k_sel = jax.vmap(lambda kb, ib: kb[ib])(k, sel)
        v_sel = jax.vmap(lambda vb, ib: vb[ib])(v, sel)
        valid = sel <= t_pos[None, :, None]
        qg = qb.reshape(B, Q_BLOCK, KVH, G, dh)
        logits = jnp.einsum('btkgd,btskd->btkgs', qg, k_sel).astype(jnp.float32) * (dh ** -0.5)
        logits = jnp.where(valid[:, :, None, None, :], logits, -jnp.inf)
        p = jax.nn.softmax(logits, axis=-1)
        o = jnp.einsum('btkgs,btskd->btkgd', p.astype(v.dtype), v_sel)
        return o.reshape(B, Q_BLOCK, H, dh)

    def to_blocks(a):
        a = a.reshape((B, nb, Q_BLOCK) + a.shape[2:])
        return jnp.moveaxis(a, 1, 0)

    t0s = jnp.arange(nb, dtype=jnp.int32) * Q_BLOCK
    out = lax.map(block, (to_blocks(q), to_blocks(q_idx), to_blocks(w_idx), t0s))
    return jnp.moveaxis(out, 0, 1).reshape(B, S, H, dh)


def spatial_gating(u, v, g_norm, w_s, b_s):
    B, S, _ = u.shape
    C = GM_CHUNK
    N = S // C
    v = layer_norm(v, g_norm)
    vc = v.reshape(B, N, C, GM_GROUPS, GM_GROUP_DIM)
    uc = u.reshape(B, N, C, GM_GROUPS, GM_GROUP_DIM)
    w_masked = w_s * jnp.tril(jnp.ones((C, C), w_s.dtype))[None]
    mixed = jnp.einsum('gts,bnsgc->bntgc', w_masked, vc) + b_s.T[None, None, :, :, None]
    return (uc * mixed).reshape(B, S, GM_W)


def setup_inputs(seed: int = 0) -> dict:
    key = jax.random.key(seed)
    ks = jax.random.split(key, 12)
    x = jax.random.normal(ks[0], (BATCH, SEQ, D_MODEL), jnp.float32)
    offset = jax.random.randint(ks[1], (BATCH, 1), 0, 1024, dtype=jnp.int32)
    positions = (offset + jnp.arange(SEQ, dtype=jnp.int32)[None, :]).astype(jnp.int32)
    norm_gain = 1.0 + 0.02 * jax.random.normal(ks[2], (DEPTH, D_MODEL), jnp.float32)
    w_in = jax.random.normal(ks[3], (DEPTH, D_MODEL, N_IN), jnp.float32) * (D_MODEL ** -0.5)
    ret_norm_gain = 1.0 + 0.02 * jax.random.normal(ks[4], (DEPTH, RET_W), jnp.float32)
    q_norm_gain = 1.0 + 0.02 * jax.random.normal(ks[5], (DEPTH, DSA_HEAD_DIM), jnp.float32)
    k_norm_gain = 1.0 + 0.02 * jax.random.normal(ks[6], (DEPTH, DSA_HEAD_DIM), jnp.float32)
    gm_norm_gain = 1.0 + 0.02 * jax.random.normal(ks[7], (DEPTH, GM_W), jnp.float32)
    w_spatial = jax.random.normal(ks[8], (DEPTH, GM_GROUPS, GM_CHUNK, GM_CHUNK), jnp.float32) * (GM_CHUNK ** -0.5)
    b_spatial = 1.0 + 0.02 * jax.random.normal(ks[9], (DEPTH, GM_GROUPS, GM_CHUNK), jnp.float32)
    w_branch = jax.random.normal(ks[10], (DEPTH, N_BRANCH, BRANCH_W, D_MODEL), jnp.float32) * (BRANCH_W ** -0.5)
    w_out = jax.random.normal(ks[11], (DEPTH, D_MODEL, D_MODEL), jnp.float32) * (D_MODEL ** -0.5)
    return {"x": x, "positions": positions, "norm_gain": norm_gain, "w_in": w_in,
            "ret_norm_gain": ret_norm_gain, "q_norm_gain": q_norm_gain, "k_norm_gain": k_norm_gain,
            "gm_norm_gain": gm_norm_gain, "w_spatial": w_spatial, "b_spatial": b_spatial,
            "w_branch": w_branch, "w_out": w_out}


def reference(x, positions, norm_gain, w_in, ret_norm_gain, q_norm_gain, k_norm_gain,
              gm_norm_gain, w_spatial, b_spatial, w_branch, w_out):
    B, S, _ = x.shape
    topk = min(TOPK_MAX, S // 4)
    split_points = np.cumsum(IN_SPLITS)[:-1].tolist()
    for l in range(DEPTH):
        xn = rms_norm(x, norm_gain[l])
        h = jnp.einsum('bsd,dn->bsn', xn, w_in[l])
        (rq, rk, rv, rg, dq, dk, dv, iq, ik, iw, dg, gu, gv, gg, mg) = jnp.split(h, split_points, axis=-1)

        ret = retention(rq.reshape(B, S, RET_HEADS, RET_HEAD_DIM),
                        rk.reshape(B, S, RET_HEADS, RET_HEAD_DIM),
                        rv.reshape(B, S, RET_HEADS, RET_HEAD_DIM), positions)
        ret = layer_norm(ret, ret_norm_gain[l].reshape(RET_HEADS, RET_HEAD_DIM)).astype(x.dtype)
        y_ret = jax.nn.silu(rg) * ret.reshape(B, S, RET_W)

        q = rope(rms_norm(dq.reshape(B, S, DSA_HEADS, DSA_HEAD_DIM), q_norm_gain[l]), positions)
        k = rope(rms_norm(dk.reshape(B, S, DSA_KV_HEADS, DSA_HEAD_DIM), k_norm_gain[l]), positions)
        v = dv.reshape(B, S, DSA_KV_HEADS, DSA_HEAD_DIM)
        qi = rope(iq.reshape(B, S, IDX_HEADS, IDX_DIM), positions)
        ki = rope(ik[:, :, None, :], positions)[:, :, 0]
        att = dsa_attention(q, k, v, qi, ki, iw, topk)
        y_dsa = jax.nn.silu(dg) * att.reshape(B, S, DSA_W)

        sg = spatial_gating(jax.nn.gelu(gu), jax.nn.gelu(gv), gm_norm_gain[l], w_spatial[l], b_spatial[l])
        y_gm = jax.nn.silu(gg) * sg

        gates = jax.nn.sigmoid(mg).reshape(B, S, N_BRANCH, D_MODEL)
        merged = (gates[:, :, 0] * jnp.einsum('bsw,wd->bsd', y_ret, w_branch[l, 0])
                  + gates[:, :, 1] * jnp.einsum('bsw,wd->bsd', y_dsa, w_branch[l, 1])
                  + gates[:, :, 2] * jnp.einsum('bsw,wd->bsd', y_gm, w_branch[l, 2]))
        x = x + jnp.einsum('bsd,de->bse', merged, w_out[l])
    return x
```

```python
import math
from contextlib import ExitStack
import numpy as np
import concourse.bass as bass
import concourse.mybir as mybir
from concourse.bass_utils import run_bass_kernel_spmd

F32 = mybir.dt.float32
BF16 = mybir.dt.bfloat16
I32 = mybir.dt.int32
ALU = mybir.AluOpType
AF = mybir.ActivationFunctionType
AX = mybir.AxisListType

D = 4096
NIN = 36000
BW = 2048
EPS = 1e-6
NBLK = D // 128


class T:
    __slots__ = ("name", "w", "r")

    def __init__(self, name=""):
        self.name = name
        self.w = set()
        self.r = set()


class Op:
    __slots__ = ("eng", "fn", "deps", "dma", "semkey", "val", "needed", "inc")


class Prog:
    ENG = ("pe", "act", "dve", "pool", "sp")

    def __init__(self, nc):
        self.nc = nc
        self.ops = {e: [] for e in self.ENG}
        self.stack = ExitStack()
        self.dma_cnt = {}
        self.out_sems = []

    def sb(self, name, shape, dt):
        return self.stack.enter_context(self.nc.sbuf_tensor(name, shape, dt))

    def ps(self, name, shape, dt=F32):
        return self.stack.enter_context(self.nc.psum_tensor(name, shape, dt))

    def op(self, eng, fn, r=(), w=(), dma=None, inc=16):
        o = Op()
        o.eng = eng
        o.fn = fn
        o.dma = dma
        o.needed = False
        o.inc = inc
        deps = set()
        for t in r:
            deps |= t.w
        for t in w:
            deps |= t.w
            deps |= t.r
        o.deps = [d for d in deps
                  if not (d.dma is None and dma is None and d.eng == "pe" and eng == "pe")]
        if dma is not None:
            c = self.dma_cnt.get(dma, 0) + inc
            self.dma_cnt[dma] = c
            o.semkey = ("D", dma)
            o.val = c
            o.needed = True
        else:
            o.semkey = ("E", eng)
            o.val = None
        for d in o.deps:
            d.needed = True
        for t in r:
            t.r.add(o)
        for t in w:
            t.w = {o}
            t.r = set()
        self.ops[eng].append(o)
        return o

    def emit(self):
        nc = self.nc
        for e in self.ENG:
            c = 0
            for o in self.ops[e]:
                if o.dma is None:
                    if o.needed:
                        c += 1
                    o.val = c if o.needed else None
        keys = set()
        for e in self.ENG:
            for o in self.ops[e]:
                keys.add(o.semkey)
        sems = {}
        for k in sorted(keys):
            sems[k] = self.stack.enter_context(nc.semaphore("s_" + "_".join(map(str, k))))
        block = self.stack.enter_context(nc.Block())
        engmap = {"pe": "tensor", "act": "scalar", "dve": "vector", "pool": "gpsimd", "sp": "sync"}
        for e in self.ENG:
            ops = self.ops[e]
            if not ops and e != "sp":
                continue

            def body(eng, ops=ops, e=e):
                waited = {}
                for o in ops:
                    need = {}
                    for d in o.deps:
                        need[d.semkey] = max(need.get(d.semkey, 0), d.val)
                    for k, v in need.items():
                        if waited.get(k, 0) < v:
                            eng.wait_ge(sems[k], v)
                            waited[k] = v
                    ins = o.fn(eng)
                    if o.dma is not None:
                        ins.then_inc(sems[o.semkey], o.inc)
                    elif o.needed:
                        ins.then_inc(sems[o.semkey], 1)
                if e == "sp":
                    for k, v in self.dma_cnt.items():
                        eng.wait_ge(sems[("D", k)], v)

            getattr(block, engmap[e])(body)
        self.stack.close()


def col_tiles():
    tiles = []
    for i in range(30):
        c0 = 512 * i
        fn = "silu" if 6144 <= c0 < 8192 else "copy"
        tiles.append((c0, 512, "T", fn, c0))
    tiles.append((15360, 160, "T", "copy", 15360))
    base = 15520
    for i in range(40):
        c0 = base + 512 * i
        if i < 4:
            tiles.append((c0, 512, "F", "silu", 0 + 512 * i))
        elif i < 8:
            tiles.append((c0, 512, "F", "gelu", 2048 + 512 * (i - 4)))
        elif i < 12:
            tiles.append((c0, 512, "T", "gelu", 15520 + 512 * (i - 8)))
        elif i < 16:
            tiles.append((c0, 512, "F", "silu", 4096 + 512 * (i - 12)))
        else:
            tiles.append((c0, 512, "F", "sigmoid", 6144 + 512 * (i - 16)))
    return tiles


HTM_W = 17568
HFM_R = 18432


def host_consts():
    c = {}
    c["ident_in"] = np.eye(128, dtype=np.float32)
    c["tril_in"] = np.tril(np.ones((128, 128), np.float32))
    half = 128
    inv = (1.0 / (10000.0 ** (np.arange(half, dtype=np.float32) * 2.0 / 256.0))).astype(np.float32)
    c["inv256_in"] = np.tile(inv[None, :], (128, 1)).astype(np.float32)
    gam = 1.0 - np.power(2.0, -5.0 - np.arange(8, dtype=np.float64))
    p = np.arange(128, dtype=np.float64)[:, None]
    c["dq_in"] = np.power(gam[None, :], p + 1.0).astype(np.float32)
    c["dk_in"] = (np.power(gam[None, :], -(p + 1.0)) / 16.0).astype(np.float32)
    c["triu_in"] = np.triu(np.ones((128, 128), np.float32))
    inv128 = (1.0 / (10000.0 ** (np.arange(64, dtype=np.float32) * 2.0 / 128.0))).astype(np.float32)
    c["inv128_in"] = np.tile(inv128[None, :], (128, 1)).astype(np.float32)
    c["negm_in"] = np.where(np.arange(128)[None, :] <= np.arange(128)[:, None], 0.0, -1e30).astype(np.float32)
    return c


def build(S, depth, GT=1024, stages=("prep", "proj"), dbg=()):
    nc = bass.Bass("TRN2", target_bir_lowering=False)
    NCH = S // 128
    GT = min(GT, S)
    NG = S // GT
    CPG = GT // 128

    def din(name, shape, dt=F32):
        return nc.dram_tensor(name, shape, dt, kind="ExternalInput").ap()

    def dscr(name, shape, dt):
        if name in dbg:
            return nc.dram_tensor(name, shape, dt, kind="ExternalOutput").ap()
        return nc.dram_tensor(name, shape, dt).ap()

    x_in = din("x", [S, D])
    pos_in = din("pos", [S, 1], I32)
    norm_gain = din("norm_gain", [depth, D])
    w_in = din("w_in", [depth, D, NIN])
    ident_in = din("ident_in", [128, 128])
    tril_in = din("tril_in", [128, 128])
    triu_in = din("triu_in", [128, 128])
    inv256_in = din("inv256_in", [128, 128])
    dq_in = din("dq_in", [128, 8])
    dk_in = din("dk_in", [128, 8])
    ret_norm_gain = din("ret_norm_gain", [depth, BW])
    q_norm_gain = din("q_norm_gain", [depth, 128])
    k_norm_gain = din("k_norm_gain", [depth, 128])
    inv128_in = din("inv128_in", [128, 64])
    negm_in = din("negm_in", [128, 128])
    gm_norm_gain = din("gm_norm_gain", [depth, BW])
    w_spatial = din("w_spatial", [depth, 16, 128, 128])
    b_spatial = din("b_spatial", [depth, 16, 128])
    w_branch = din("w_branch", [depth, 3, BW, D])
    w_out = din("w_out", [depth, D, D])
    out = nc.dram_tensor("out", [S, D], F32, kind="ExternalOutput").ap()

    tiles = col_tiles()
    wt = [dscr("wt%d" % ti, [128, NBLK, ncol], BF16) for ti, (c0, ncol, kind, fn, dc) in enumerate(tiles)]
    h_tm = [dscr("h_tm%d" % g, [GT, HTM_W], BF16) for g in range(NG)]
    h_fm = [dscr("h_fm%d" % g, [HFM_R, GT], BF16) for g in range(NG)]
    wbr = [dscr("wbr%d" % b, [32, 128, 16, 128], BF16) for b in range(3)]
    wo = dscr("wo", [8, 128, NBLK, 512], BF16)
    yT = [dscr("yT%d" % b, [BW, S], BF16) for b in range(3)]
    xmid = dscr("xmid", [S, D], F32)
    kT_all = dscr("kT_all", [128, 4, S], BF16)
    kiT_all = dscr("kiT_all", [128, S], BF16)
    v_all = dscr("v_all", [S, 512], BF16)

    P = Prog(nc)
    xT = P.sb("xT", [128, NBLK, max(GT, 1024)], BF16)
    xTflat = xT[:].rearrange("p a b -> p (a b)")
    wtile = [P.sb("wtile%d" % i, [128, NBLK, 512], BF16) for i in range(2)]
    stg_f = [P.sb("stgf%d" % i, [128, 8, 512], F32) for i in range(2)]
    stg_b = [P.sb("stgb%d" % i, [128, 8, 512], BF16) for i in range(2)]
    ev_t = [P.sb("evt%d" % i, [128, 512], F32) for i in range(2)]
    ev_o = [P.sb("evo%d" % i, [128, 512], BF16) for i in range(4)]
    gB = P.sb("gB", [128, D], F32)
    ident_f = P.sb("ident_f", [128, 128], F32)
    ident = P.sb("ident", [128, 128], BF16)
    small = P.sb("small", [128, 64], F32)
    gmB = xTflat[:, 0:4096].bitcast(F32)
    WmT = xTflat[:, 18432:20480].rearrange("p (g t) -> p g t", g=16)
    bsp = xTflat[0:1, 20480:22528]
    ones_row = P.sb("ones_row", [1, 128], BF16)
    trilm = P.sb("trilm", [128, 128], F32)
    ps_g = [P.ps("psg%d" % i, [128, 512], F32) for i in range(2)]
    ps_t = [P.ps("pst%d" % i, [128, 1024], BF16) for i in range(2)]
    ps_x = [P.ps("psx%d" % i, [128, 512], F32) for i in range(2)]
    t_psx = [T(), T()]
    triu = P.sb("triu", [128, 128], F32)
    inv256 = P.sb("inv256", [128, 128], F32)
    dqk = P.sb("dqk", [128, 16], F32)
    posi = P.sb("posi", [128, 2], I32)
    kint = P.sb("kint", [128, 128], I32)
    t_kint = T()
    t_triu, t_inv, t_dqk, t_posi = T(), T(), T(), T()
    inv128 = P.sb("inv128", [128, 64], F32)
    negm = P.sb("negm", [128, 128], F32)
    gqk = P.sb("gqk", [128, 256], F32)
    ones_bf = P.sb("ones_bf", [128, 128], BF16)
    sm2 = P.sb("sm2", [128, 160], F32)
    t_inv128, t_negm, t_gqk, t_onesbf, t_sm2 = T(), T(), T(), T(), T()
    P.op("sp", lambda e: e.dma_start(out=inv128[:], in_=inv128_in), w=[t_inv128], dma="cst")
    P.op("sp", lambda e: e.dma_start(out=negm[:], in_=negm_in), w=[t_negm], dma="cst")
    P.op("pool", lambda e: e.memset(ones_bf[:], 1.0), w=[t_onesbf])
    t_kpre = [T() for _ in range(NCH)]
    t_kv = [T(), T()]
    t_kit1 = T()
    P.op("sp", lambda e: e.dma_start(out=triu[:], in_=triu_in), w=[t_triu], dma="cst")
    P.op("sp", lambda e: e.dma_start(out=inv256[:], in_=inv256_in), w=[t_inv], dma="cst")
    P.op("sp", lambda e: e.dma_start(out=dqk[:, 0:8], in_=dq_in), w=[t_dqk], dma="cst")
    P.op("sp", lambda e: e.dma_start(out=dqk[:, 8:16], in_=dk_in), w=[t_dqk], dma="cst")

    t_xT = [T("xT%d" % i) for i in range(CPG)]
    t_wtile = [T(), T()]
    t_stgf = [T(), T()]
    t_stgb = [T(), T()]
    t_evt = [T(), T()]
    t_evo = [T() for _ in range(4)]
    t_gB = T()
    t_ident = T()
    t_identf = T()
    t_small = T()
    t_psg = [T(), T()]
    t_pst = [T(), T()]
    t_wt = [T() for _ in tiles]
    t_htm = [T() for _ in range(NCH)]
    t_hfm = [T() for _ in range(NG)]
    t_xcur = [T() for _ in range(NCH)]
    t_WmT, t_bsp, t_bspf, t_ones, t_tril = T(), T(), T(), T(), T()
    t_wbr = [[T() for _ in range(32)] for _ in range(3)]
    t_wo = [T() for _ in range(8)]
    t_yT = [[T() for _ in range(NCH)] for _ in range(3)]
    t_xnext = [T() for _ in range(NCH)]
    P.op("sp", lambda e: e.dma_start(out=trilm[:], in_=tril_in), w=[t_tril], dma="tril")
    P.op("pool", lambda e: e.memset(ones_row[:], 1.0), w=[t_ones])

    P.op("sp", lambda e: e.dma_start(out=ident_f[:], in_=ident_in), w=[t_identf], dma="identf")
    P.op("dve", lambda e: e.tensor_copy(out=ident[:], in_=ident_f[:]), r=[t_identf], w=[t_ident])

    cnt = {"stg": 0, "ev": 0, "psg": 0, "pst": 0, "w": 0}

    tab256 = dscr("tab256", [NCH, 128, 256], F32)
    tab128 = dscr("tab128", [NCH, 128, 128], F32)
    t_tab256 = [T() for _ in range(NCH)]
    t_tab128 = [T() for _ in range(NCH)]
    if "ret" in stages or "dsa" in stages:
        PI = math.pi
        PI2 = math.pi
        ang, cosT, sinT, atmp = [ev_t[1][:, q * 128:(q + 1) * 128] for q in range(4)]
        kf = ev_t[0][:, 0:128]
        rt = ev_t[1]
        ang2, cos2, sin2, atmp2 = rt[:, 0:64], rt[:, 64:128], rt[:, 128:192], rt[:, 192:256]
        kf2 = rt[:, 256:320]
        kint64 = kint[:, 0:64]
        for c in range(NCH):
            P.op("sp", lambda e, c=c: e.dma_start(out=posi[:, 0:1], in_=pos_in[c * 128:(c + 1) * 128, :]), w=[t_posi], dma="posi")
            posf = small[:, 20:21]
            P.op("dve", lambda e, posf=posf: e.tensor_copy(out=posf, in_=posi[:, 0:1]), r=[t_posi], w=[t_small])
            P.op("dve", lambda e, posf=posf: e.tensor_scalar(out=ang, in0=inv256[:], scalar1=posf, scalar2=None, op0=ALU.mult),
                 r=[t_inv, t_small], w=[t_evt[1]])
            kf = ev_t[0][:, 0:128]
            for (dst, shift) in ((sinT, PI), (cosT, 1.5 * PI)):
                rw2 = dict(r=[t_evt[1], t_evt[0], t_kint], w=[t_evt[1], t_evt[0], t_kint])
                P.op("dve", lambda e, shift=shift: e.tensor_scalar(out=atmp, in0=ang, scalar1=1.0 / (2 * PI), scalar2=shift / (2 * PI),
                                                                  op0=ALU.mult, op1=ALU.add), **rw2)
                P.op("dve", lambda e: e.tensor_copy(out=kint[:], in_=atmp), **rw2)
                P.op("dve", lambda e: e.tensor_copy(out=kf, in_=kint[:]), **rw2)
                P.op("dve", lambda e: e.tensor_tensor(out=atmp, in0=atmp, in1=kf, op=ALU.subtract), **rw2)
                P.op("dve", lambda e: e.tensor_scalar(out=kf, in0=atmp, scalar1=0.0, scalar2=None, op0=ALU.is_lt), **rw2)
                P.op("dve", lambda e: e.tensor_tensor(out=atmp, in0=atmp, in1=kf, op=ALU.add), **rw2)
                P.op("dve", lambda e: e.tensor_scalar(out=atmp, in0=atmp, scalar1=2 * PI, scalar2=-PI, op0=ALU.mult, op1=ALU.add), **rw2)
                P.op("dve", lambda e: e.tensor_scalar(out=atmp, in0=atmp, scalar1=-PI, scalar2=PI, op0=ALU.max, op1=ALU.min), **rw2)
                P.op("act", lambda e, dst=dst: e.activation(out=dst, in_=atmp, func=AF.Sin), r=[t_evt[1]], w=[t_evt[1]])
            P.op("pool", lambda e, c=c: e.dma_start(out=tab256[c], in_=ev_t[1][:, 128:384]), r=[t_evt[1]], w=[t_tab256[c]], dma="tabst")
            P.op("sp", lambda e, c=c: e.dma_start(out=posi[:, 0:1], in_=pos_in[c * 128:(c + 1) * 128, :]), w=[t_posi], dma="posi")
            posf = small[:, 20:21]
            P.op("dve", lambda e, posf=posf: e.tensor_copy(out=posf, in_=posi[:, 0:1]), r=[t_posi], w=[t_small])
            P.op("dve", lambda e, posf=posf: e.tensor_scalar(out=ang2, in0=inv128[:], scalar1=posf, scalar2=None, op0=ALU.mult),
                 r=[t_inv128, t_small], w=[t_evt[1]])
            for (dst, shift) in ((sin2, PI2), (cos2, 1.5 * PI2)):
                rw2 = dict(r=[t_evt[1], t_kint], w=[t_evt[1], t_kint])
                P.op("dve", lambda e, shift=shift: e.tensor_scalar(out=atmp2, in0=ang2, scalar1=1.0 / (2 * PI2), scalar2=shift / (2 * PI2),
                                                                  op0=ALU.mult, op1=ALU.add), **rw2)
                P.op("dve", lambda e: e.tensor_copy(out=kint64, in_=atmp2), **rw2)
                P.op("dve", lambda e: e.tensor_copy(out=kf2, in_=kint64), **rw2)
                P.op("dve", lambda e: e.tensor_tensor(out=atmp2, in0=atmp2, in1=kf2, op=ALU.subtract), **rw2)
                P.op("dve", lambda e: e.tensor_scalar(out=kf2, in0=atmp2, scalar1=0.0, scalar2=None, op0=ALU.is_lt), **rw2)
                P.op("dve", lambda e: e.tensor_tensor(out=atmp2, in0=atmp2, in1=kf2, op=ALU.add), **rw2)
                P.op("dve", lambda e: e.tensor_scalar(out=atmp2, in0=atmp2, scalar1=2 * PI2, scalar2=-PI2, op0=ALU.mult, op1=ALU.add), **rw2)
                P.op("dve", lambda e: e.tensor_scalar(out=atmp2, in0=atmp2, scalar1=-PI2, scalar2=PI2, op0=ALU.max, op1=ALU.min), **rw2)
                P.op("act", lambda e, dst=dst: e.activation(out=dst, in_=atmp2, func=AF.Sin), r=[t_evt[1]], w=[t_evt[1]])
            P.op("pool", lambda e, c=c: e.dma_start(out=tab128[c], in_=ev_t[1][:, 64:192]), r=[t_evt[1]], w=[t_tab128[c]], dma="tabst")
    for l in range(depth):
        x_cur = x_in if l == 0 else xmid
        x_next = out if l == depth - 1 else xmid
        if l > 0:
            t_xcur = t_xnext
            t_xnext = [T() for _ in range(NCH)]
        if "prep" in stages:
            wv = w_in[l].rearrange("(k p) n -> p k n", p=128)
            for ti, (c0, ncol, kind, fn, dc) in enumerate(tiles):
                dstv = wt[ti]
                for kq in range(4):
                    i = cnt["stg"] % 2
                    cnt["stg"] += 1
                    P.op("sp", lambda e, i=i, kq=kq, c0=c0, ncol=ncol, wv=wv: e.dma_start(
                        out=stg_f[i][:, :, 0:ncol], in_=wv[:, kq * 8:(kq + 1) * 8, c0:c0 + ncol]),
                        w=[t_stgf[i]], dma="stgf%d" % i)
                    if kq % 2 == 0:
                        P.op("dve", lambda e, i=i, ncol=ncol: e.tensor_copy(out=stg_b[i][:, :, 0:ncol], in_=stg_f[i][:, :, 0:ncol]),
                             r=[t_stgf[i]], w=[t_stgb[i]])
                    else:
                        P.op("act", lambda e, i=i, ncol=ncol: e.activation(out=stg_b[i][:, :, 0:ncol], in_=stg_f[i][:, :, 0:ncol], func=AF.Copy),
                             r=[t_stgf[i]], w=[t_stgb[i]])
                    P.op("pool", lambda e, i=i, kq=kq, dstv=dstv, ncol=ncol: e.dma_start(
                        out=dstv[:, kq * 8:(kq + 1) * 8, :], in_=stg_b[i][:, :, 0:ncol]),
                        r=[t_stgb[i]], w=[t_wt[ti]], dma="stgb%d" % i)
        if "proj" in stages:
            P.op("sp", lambda e, l=l: e.dma_start(out=gB[:], in_=norm_gain[l:l + 1, :].partition_broadcast(128)),
                 w=[t_gB], dma="gB")
            for g in range(NG):
                for ch in range(CPG):
                    c = g * CPG + ch
                    i = cnt["stg"] % 2
                    cnt["stg"] += 1
                    xf = stg_f[i][:].rearrange("p a b -> p (a b)")
                    xs = stg_b[i][:].rearrange("p a b -> p (a b)")
                    P.op("sp", lambda e, c=c, xf=xf, x_cur=x_cur: e.dma_start(out=xf, in_=x_cur[c * 128:(c + 1) * 128, :]),
                         r=[t_xcur[c]], w=[t_stgf[i]], dma="stgf%d" % i)
                    ss = small[:, 0:1]
                    rs = small[:, 1:2]
                    P.op("act", lambda e, xf=xf, xs=xs, ss=ss: e.activation(out=xs, in_=xf, func=AF.Square, accum_out=ss),
                         r=[t_stgf[i]], w=[t_stgb[i], t_small])
                    P.op("dve", lambda e, ss=ss, rs=rs: e.tensor_scalar(out=rs, in0=ss, scalar1=1.0 / D, scalar2=EPS,
                                                                       op0=ALU.mult, op1=ALU.add), r=[t_small], w=[t_small])
                    P.op("act", lambda e, rs=rs: e.activation(out=rs, in_=rs, func=AF.Sqrt), r=[t_small], w=[t_small])
                    P.op("dve", lambda e, rs=rs: e.reciprocal(out=rs, in_=rs), r=[t_small], w=[t_small])
                    P.op("dve", lambda e, xf=xf, xs=xs, rs=rs: e.scalar_tensor_tensor(
                        out=xs, in0=xf, scalar=rs, in1=gB[:], op0=ALU.mult, op1=ALU.mult),
                        r=[t_stgf[i], t_small, t_gB], w=[t_stgb[i]])
                    for k8 in range(4):
                        j = cnt["pst"] % 2
                        cnt["pst"] += 1
                        for kk in range(8):
                            k = k8 * 8 + kk
                            P.op("pe", lambda e, j=j, kk=kk, k=k, xs=xs: e.transpose(
                                out=ps_t[j][:, kk * 128:(kk + 1) * 128], in_=xs[:, k * 128:(k + 1) * 128], identity=ident[:]),
                                r=[t_stgb[i], t_ident], w=[t_pst[j]])
                        ceng = ("act", "dve")[k8 % 2]
                        if ceng == "act":
                            P.op("act", lambda e, j=j, k8=k8, ch=ch: e.activation(
                                out=xT[:, k8 * 8:(k8 + 1) * 8, ch * 128:(ch + 1) * 128],
                                in_=ps_t[j][:].rearrange("p (a b) -> p a b", a=8), func=AF.Copy),
                                r=[t_pst[j]], w=[t_xT[ch]])
                        else:
                            P.op("dve", lambda e, j=j, k8=k8, ch=ch: e.tensor_copy(
                                out=xT[:, k8 * 8:(k8 + 1) * 8, ch * 128:(ch + 1) * 128],
                                in_=ps_t[j][:].rearrange("p (a b) -> p a b", a=8)),
                                r=[t_pst[j]], w=[t_xT[ch]])
                for ti, (c0, ncol, kind, fn, dc) in enumerate(tiles):
                    wi = cnt["w"] % 2
                    cnt["w"] += 1
                    srcv = wt[ti]
                    P.op("sp", lambda e, wi=wi, srcv=srcv, ncol=ncol: e.dma_start(out=wtile[wi][:, :, 0:ncol], in_=srcv),
                         r=[t_wt[ti]], w=[t_wtile[wi]], dma="wtile%d" % wi)
                    if kind == "T":
                        units = [("T", ch, 0) for ch in range(CPG)]
                    else:
                        units = [("F", tt, cb) for cb in range(ncol // 128) for tt in range(GT // 512)]
                    for (kd, a, b) in units:
                        pj = cnt["psg"] % 2
                        cnt["psg"] += 1
                        if kd == "T":
                            pv = ps_g[pj][:, 0:ncol]
                            for k in range(NBLK):
                                P.op("pe", lambda e, pv=pv, k=k, a=a, wi=wi, ncol=ncol: e.matmul(
                                    pv, lhsT=xT[:, k, a * 128:(a + 1) * 128], rhs=wtile[wi][:, k, 0:ncol],
                                    start=(k == 0), stop=(k == NBLK - 1)),
                                    r=[t_xT[a], t_wtile[wi]], w=[t_psg[pj]])
                            n_el = ncol
                        else:
                            pv = ps_g[pj][:, :]
                            chs = [a * 4 + q for q in range(4)]
                            for k in range(NBLK):
                                P.op("pe", lambda e, pv=pv, k=k, a=a, b=b, wi=wi: e.matmul(
                                    pv, lhsT=wtile[wi][:, k, b * 128:(b + 1) * 128], rhs=xT[:, k, a * 512:(a + 1) * 512],
                                    start=(k == 0), stop=(k == NBLK - 1)),
                                    r=[t_xT[q] for q in chs] + [t_wtile[wi]], w=[t_psg[pj]])
                            n_el = 512
                        oi = cnt["ev"] % 4
                        cnt["ev"] += 1
                        ov = ev_o[oi][:, 0:n_el]
                        if fn in ("copy", "silu", "sigmoid"):
                            func = {"copy": AF.Copy, "silu": AF.Silu, "sigmoid": AF.Sigmoid}[fn]
                            P.op("act", lambda e, ov=ov, pv=pv, func=func: e.activation(out=ov, in_=pv, func=func),
                                 r=[t_psg[pj]], w=[t_evo[oi]])
                        else:
                            ti2 = oi % 2
                            tv = ev_t[ti2][:, 0:n_el]
                            P.op("act", lambda e, tv=tv, pv=pv: e.activation(out=tv, in_=pv, func=AF.Square),
                                 r=[t_psg[pj]], w=[t_evt[ti2]])
                            P.op("dve", lambda e, tv=tv: e.tensor_scalar(out=tv, in0=tv, scalar1=0.044715, scalar2=1.0,
                                                                        op0=ALU.mult, op1=ALU.add), r=[t_evt[ti2]], w=[t_evt[ti2]])
                            P.op("dve", lambda e, tv=tv, pv=pv: e.tensor_tensor(out=tv, in0=tv, in1=pv, op=ALU.mult),
                                 r=[t_evt[ti2], t_psg[pj]], w=[t_evt[ti2]])
                            P.op("act", lambda e, tv=tv: e.activation(out=tv, in_=tv, func=AF.Sigmoid, scale=1.5957691216057308),
                                 r=[t_evt[ti2]], w=[t_evt[ti2]])
                            P.op("dve", lambda e, tv=tv, pv=pv, ov=ov: e.tensor_tensor(out=ov, in0=tv, in1=pv, op=ALU.mult),
                                 r=[t_evt[ti2], t_psg[pj]], w=[t_evo[oi]])
                        if kd == "T":
                            c = g * CPG + a
                            P.op("pool", lambda e, ov=ov, a=a, g=g, dc=dc, ncol=ncol: e.dma_start(
                                out=h_tm[g][a * 128:(a + 1) * 128, dc:dc + ncol], in_=ov),
                                r=[t_evo[oi]], w=[t_htm[c]], dma="evo%d" % oi)
                        else:
                            r0 = dc + b * 128
                            t0 = a * 512
                            P.op("pool", lambda e, ov=ov, r0=r0, t0=t0, g=g: e.dma_start(
                                out=h_fm[g][r0:r0 + 128, t0:t0 + 512], in_=ov),
                                r=[t_evo[oi]], w=[t_hfm[g]], dma="evo%d" % oi)

        if "back" in stages:
            for b in range(3):
                wv = w_branch[l, b].rearrange("(k p) n -> p k n", p=128)
                for kh in range(2):
                    for ct in range(8):
                        i = cnt["stg"] % 2
                        cnt["stg"] += 1
                        P.op("sp", lambda e, i=i, wv=wv, kh=kh, ct=ct: e.dma_start(
                            out=stg_f[i][:], in_=wv[:, kh * 8:(kh + 1) * 8, ct * 512:(ct + 1) * 512]),
                            w=[t_stgf[i]], dma="stgf%d" % i)
                        if ct % 2 == 0:
                            P.op("dve", lambda e, i=i: e.tensor_copy(out=stg_b[i][:], in_=stg_f[i][:]), r=[t_stgf[i]], w=[t_stgb[i]])
                        else:
                            P.op("act", lambda e, i=i: e.activation(out=stg_b[i][:], in_=stg_f[i][:], func=AF.Copy), r=[t_stgf[i]], w=[t_stgb[i]])
                        for e4 in range(4):
                            P.op("pool", lambda e, i=i, b=b, kh=kh, ct=ct, e4=e4: e.dma_start(
                                out=wbr[b][ct * 4 + e4, :, kh * 8:(kh + 1) * 8, :], in_=stg_b[i][:, :, e4 * 128:(e4 + 1) * 128]),
                                r=[t_stgb[i]], w=[t_wbr[b][ct * 4 + e4]], dma="stgb%d" % i)
            wv = w_out[l].rearrange("(k p) n -> p k n", p=128)
            for kq in range(4):
                for ct in range(8):
                    i = cnt["stg"] % 2
                    cnt["stg"] += 1
                    P.op("sp", lambda e, i=i, wv=wv, kq=kq, ct=ct: e.dma_start(
                        out=stg_f[i][:], in_=wv[:, kq * 8:(kq + 1) * 8, ct * 512:(ct + 1) * 512]),
                        w=[t_stgf[i]], dma="stgf%d" % i)
                    if ct % 2 == 0:
                        P.op("dve", lambda e, i=i: e.tensor_copy(out=stg_b[i][:], in_=stg_f[i][:]), r=[t_stgf[i]], w=[t_stgb[i]])
                    else:
                        P.op("act", lambda e, i=i: e.activation(out=stg_b[i][:], in_=stg_f[i][:], func=AF.Copy), r=[t_stgf[i]], w=[t_stgb[i]])
                    P.op("pool", lambda e, i=i, kq=kq, ct=ct: e.dma_start(
                        out=wo[ct, :, kq * 8:(kq + 1) * 8, :], in_=stg_b[i][:]),
                        r=[t_stgb[i]], w=[t_wo[ct]], dma="stgb%d" % i)
        if "gmlp" in stages:
            wsf = stg_f[0][:].rearrange("p a b -> p (a b)")[:, 0:2048].rearrange("p (g s) -> p g s", g=16)
            wsb = stg_b[0][:].rearrange("p a b -> p (a b)")[:, 0:2048].rearrange("p (g s) -> p g s", g=16)
            P.op("sp", lambda e, l=l, wsf=wsf: e.dma_start(out=wsf, in_=w_spatial[l].rearrange("g t s -> t g s")),
                 w=[t_stgf[0]], dma="stgf0")
            P.op("dve", lambda e, wsf=wsf, wsb=wsb: e.tensor_tensor(
                out=wsb, in0=wsf, in1=trilm[:].unsqueeze(1).to_broadcast([128, 16, 128]), op=ALU.mult),
                r=[t_stgf[0], t_tril], w=[t_stgb[0]])
            for half in range(2):
                j = cnt["pst"] % 2
                cnt["pst"] += 1
                for gg_ in range(8):
                    gi = half * 8 + gg_
                    P.op("pe", lambda e, j=j, gg_=gg_, gi=gi, wsb=wsb: e.transpose(
                        out=ps_t[j][:, gg_ * 128:(gg_ + 1) * 128], in_=wsb[:, gi, :], identity=ident[:]),
                        r=[t_stgb[0], t_ident], w=[t_pst[j]])
                P.op("dve", lambda e, j=j, half=half: e.tensor_copy(
                    out=WmT[:, half * 8:(half + 1) * 8, :], in_=ps_t[j][:].rearrange("p (a b) -> p a b", a=8)),
                    r=[t_pst[j]], w=t_xT)
            bsp_f = stg_f[1][0:1].rearrange("p a b -> p (a b)")[:, 0:BW]
            P.op("sp", lambda e, l=l, bsp_f=bsp_f: e.dma_start(out=bsp_f, in_=b_spatial[l:l + 1].rearrange("o g t -> o (g t)")),
                 w=[t_stgf[1]], dma="stgf1")
            P.op("dve", lambda e, bsp_f=bsp_f: e.tensor_copy(out=bsp, in_=bsp_f), r=[t_stgf[1]], w=t_xT)
            P.op("sp", lambda e, l=l: e.dma_start(out=gmB, in_=gm_norm_gain[l:l + 1, :].partition_broadcast(128)),
                 w=t_xT, dma="gmB")
            for c in range(NCH):
                g, a = divmod(c, CPG)
                vt = stg_b[0][:].rearrange("p a b -> p (a b)")[:, 0:2048]
                vn = stg_b[1][:].rearrange("p a b -> p (a b)")[:, 0:2048]
                tf = stg_f[0][:].rearrange("p a b -> p (a b)")[:, 0:2048]
                P.op("sp", lambda e, g=g, a=a, vt=vt: e.dma_start(out=vt, in_=h_tm[g][a * 128:(a + 1) * 128, 15520:17568]),
                     r=[t_htm[c]], w=[t_stgb[0]], dma="stgb0")
                sm, sq, mean, var, rstd, nb = [small[:, 8 + q:9 + q] for q in range(6)]
                P.op("act", lambda e, vt=vt, tf=tf, sm=sm: e.activation(out=tf, in_=vt, func=AF.Copy, accum_out=sm),
                     r=[t_stgb[0]], w=[t_stgf[0], t_small])
                P.op("act", lambda e, vt=vt, tf=tf, sq=sq: e.activation(out=tf, in_=vt, func=AF.Square, accum_out=sq),
                     r=[t_stgb[0]], w=[t_stgf[0], t_small])
                P.op("dve", lambda e, sm=sm, mean=mean: e.tensor_scalar(out=mean, in0=sm, scalar1=1.0 / BW, scalar2=None, op0=ALU.mult),
                     r=[t_small], w=[t_small])
                P.op("dve", lambda e, mean=mean, var=var: e.tensor_tensor(out=var, in0=mean, in1=mean, op=ALU.mult),
                     r=[t_small], w=[t_small])
                P.op("dve", lambda e, sq=sq, var=var: e.scalar_tensor_tensor(out=var, in0=sq, scalar=1.0 / BW, in1=var,
                                                                             op0=ALU.mult, op1=ALU.subtract),
                     r=[t_small], w=[t_small])
                P.op("dve", lambda e, var=var, rstd=rstd: e.tensor_scalar(out=rstd, in0=var, scalar1=EPS, scalar2=None, op0=ALU.add),
                     r=[t_small], w=[t_small])
                P.op("act", lambda e, rstd=rstd: e.activation(out=rstd, in_=rstd, func=AF.Sqrt), r=[t_small], w=[t_small])
                P.op("dve", lambda e, rstd=rstd: e.reciprocal(out=rstd, in_=rstd), r=[t_small], w=[t_small])
                P.op("dve", lambda e, mean=mean, rstd=rstd, nb=nb: e.scalar_tensor_tensor(
                    out=nb, in0=mean, scalar=-1.0, in1=rstd, op0=ALU.mult, op1=ALU.mult), r=[t_small], w=[t_small])
                P.op("act", lambda e, vt=vt, tf=tf, rstd=rstd, nb=nb: e.activation(out=tf, in_=vt, func=AF.Identity, bias=nb, scale=rstd),
                     r=[t_stgb[0], t_small], w=[t_stgf[0]])
                P.op("dve", lambda e, tf=tf, vn=vn: e.tensor_tensor(out=vn, in0=tf, in1=gmB, op=ALU.mult),
                     r=[t_stgf[0]] + t_xT, w=[t_stgb[1]])
                for q4 in range(4):
                    pj = cnt["psg"] % 2
                    cnt["psg"] += 1
                    for gq in range(4):
                        gi = q4 * 4 + gq
                        pv = ps_g[pj][:, gq * 128:(gq + 1) * 128]
                        P.op("pe", lambda e, pv=pv, gi=gi, vn=vn: e.matmul(pv, lhsT=vn[:, gi * 128:(gi + 1) * 128], rhs=WmT[:, gi, :],
                                                                         start=True, stop=False),
                             r=[t_stgb[1]] + t_xT, w=[t_psg[pj]])
                        P.op("pe", lambda e, pv=pv, gi=gi: e.matmul(pv, lhsT=ones_row[:], rhs=bsp[:, gi * 128:(gi + 1) * 128],
                                                                  start=False, stop=True),
                             r=[t_ones] + t_xT, w=[t_psg[pj]])
                    P.op("sp", lambda e, g=g, a=a, q4=q4: e.dma_start(
                        out=ev_o[0][:].rearrange("p (q t) -> p q t", q=4),
                        in_=h_fm[g][2048 + q4 * 512: 2048 + (q4 + 1) * 512, a * 128:(a + 1) * 128].rearrange("(q c) t -> c q t", c=128)),
                        r=[t_hfm[g]], w=[t_evo[0]], dma="evo0")
                    P.op("sp", lambda e, g=g, a=a, q4=q4: e.dma_start(
                        out=ev_o[1][:].rearrange("p (q t) -> p q t", q=4),
                        in_=h_fm[g][4096 + q4 * 512: 4096 + (q4 + 1) * 512, a * 128:(a + 1) * 128].rearrange("(q c) t -> c q t", c=128)),
                        r=[t_hfm[g]], w=[t_evo[1]], dma="evo1")
                    P.op("dve", lambda e, pj=pj: e.tensor_tensor(out=ev_t[0][:], in0=ps_g[pj][:], in1=ev_o[0][:], op=ALU.mult),
                         r=[t_psg[pj], t_evo[0]], w=[t_evt[0]])
                    P.op("dve", lambda e: e.tensor_tensor(out=ev_o[2][:], in0=ev_t[0][:], in1=ev_o[1][:], op=ALU.mult),
                         r=[t_evt[0], t_evo[1]], w=[t_evo[2]])
                    P.op("pool", lambda e, c=c, q4=q4: e.dma_start(
                        out=yT[2][q4 * 512:(q4 + 1) * 512, c * 128:(c + 1) * 128].rearrange("(q c) t -> c q t", c=128),
                        in_=ev_o[2][:].rearrange("p (q t) -> p q t", q=4)),
                        r=[t_evo[2]], w=[t_yT[2][c]], dma="evo2")

        if "ret" in stages:
            PI = math.pi
            wt1 = wtile[1][:].rearrange("p a b -> p (a b)")
            wt0 = wtile[0][:].rearrange("p a b -> p (a b)")
            hq = wt1[:, 0:8192]
            ybf = wt1[:, 8192:10240]
            yTs = wt1[:, 10240:12288]
            sf0 = stg_f[0][:].rearrange("p a b -> p (a b)")
            t1 = sf0[:, 0:1024].rearrange("p (h d) -> p h d", h=8)
            t2 = sf0[:, 1024:2048].rearrange("p (h d) -> p h d", h=8)
            ro = sf0[:, 2048:4096].rearrange("p (h d) -> p h d", h=8)
            Tst = stg_f[1][:].rearrange("p a b -> p (a b)").rearrange("p (h k e) -> p h k e", h=8, k=2)
            Rb = wt0[:, 0:4096].rearrange("p (h k e) -> p h k e", h=8, k=2)
            sb0 = stg_b[0][:].rearrange("p a b -> p (a b)")
            sb1 = stg_b[1][:].rearrange("p a b -> p (a b)")
            qk_t = [sb0[:, 0:2048], sb0[:, 2048:4096]]
            qkT = [sb1[:, 0:2048].rearrange("p (j t) -> p j t", j=16), sb1[:, 2048:4096].rearrange("p (j t) -> p j t", j=16)]
            ret_f = gB[:, 0:2048]
            sqt = gB[:, 2048:4096]
            ang, cosT, sinT, atmp = [ev_t[1][:, q * 128:(q + 1) * 128] for q in range(4)]
            gamC = [float((1.0 - 2.0 ** (-5.0 - h)) ** 128) for h in range(8)]
            P.op("sp", lambda e, l=l: e.dma_start(out=gmB, in_=ret_norm_gain[l:l + 1, :].partition_broadcast(128)),
                 w=t_xT, dma="gmB")
            P.op("pool", lambda e: e.memset(stg_f[1][:], 0.0), w=[t_stgf[1]])
            P.op("pool", lambda e: e.memset(wt0[:, 0:4096], 0.0), w=[t_wtile[0]])
            for c in range(NCH):
                g, a = divmod(c, CPG)
                P.op("sp", lambda e, g=g, a=a: e.dma_start(out=hq, in_=h_tm[g][a * 128:(a + 1) * 128, 0:8192]),
                     r=[t_htm[c]], w=[t_wtile[1]], dma="wtile1")
                P.op("sp", lambda e, c=c: e.dma_start(out=ev_t[1][:, 128:384], in_=tab256[c]), r=[t_tab256[c]], w=[t_evt[1]], dma="tabld")
                cosb = cosT.unsqueeze(1).to_broadcast([128, 8, 128])
                sinb = sinT.unsqueeze(1).to_broadcast([128, 8, 128])
                for qi in range(2):
                    src = hq[:, qi * 2048:(qi + 1) * 2048].rearrange("p (h d) -> p h d", h=8)
                    x1 = src[:, :, 0:128]
                    x2 = src[:, :, 128:256]
                    dec = dqk[:, qi * 8:(qi + 1) * 8].unsqueeze(2).to_broadcast([128, 8, 256])
                    rw = dict(r=[t_wtile[1], t_evt[1], t_stgf[0]], w=[t_stgf[0]])
                    P.op("dve", lambda e, x1=x1: e.tensor_tensor(out=t1, in0=x1, in1=cosb, op=ALU.mult), **rw)
                    P.op("dve", lambda e, x2=x2: e.tensor_tensor(out=t2, in0=x2, in1=sinb, op=ALU.mult), **rw)
                    P.op("dve", lambda e: e.tensor_tensor(out=ro[:, :, 0:128], in0=t1, in1=t2, op=ALU.subtract), **rw)
                    P.op("dve", lambda e, x2=x2: e.tensor_tensor(out=t1, in0=x2, in1=cosb, op=ALU.mult), **rw)
                    P.op("dve", lambda e, x1=x1: e.tensor_tensor(out=t2, in0=x1, in1=sinb, op=ALU.mult), **rw)
                    P.op("dve", lambda e: e.tensor_tensor(out=ro[:, :, 128:256], in0=t1, in1=t2, op=ALU.add), **rw)
                    P.op("dve", lambda e, qi=qi, dec=dec: e.tensor_tensor(
                        out=qk_t[qi].rearrange("p (h d) -> p h d", h=8), in0=ro, in1=dec, op=ALU.mult),
                        r=[t_stgf[0], t_dqk], w=[t_stgb[0]])
                    for half in range(2):
                        j = cnt["pst"] % 2
                        cnt["pst"] += 1
                        for jj in range(8):
                            bi = half * 8 + jj
                            P.op("pe", lambda e, j=j, jj=jj, bi=bi, qi=qi: e.transpose(
                                out=ps_t[j][:, jj * 128:(jj + 1) * 128], in_=qk_t[qi][:, bi * 128:(bi + 1) * 128], identity=ident[:]),
                                r=[t_stgb[0], t_ident], w=[t_pst[j]])
                        P.op("act", lambda e, j=j, half=half, qi=qi: e.activation(
                            out=qkT[qi][:, half * 8:(half + 1) * 8, :], in_=ps_t[j][:].rearrange("p (a b) -> p a b", a=8), func=AF.Copy),
                            r=[t_pst[j]], w=[t_stgb[1]])
                qT, kT = qkT

                def ret_pa(h):
                    pa = ps_x[h % 2][:, 0:128]
                    for dbl in range(2):
                        P.op("pe", lambda e, h=h, dbl=dbl, pa=pa: e.matmul(pa, lhsT=kT[:, 2 * h + dbl, :], rhs=qT[:, 2 * h + dbl, :],
                                                                          start=(dbl == 0), stop=(dbl == 1)),
                             r=[t_stgb[1]], w=[t_psx[h % 2]])

                ret_pa(0)
                for h in range(8):
                    if h + 1 < 8:
                        ret_pa(h + 1)
                    vh = hq[:, 4096 + h * 256: 4096 + (h + 1) * 256]
                    pa = ps_x[h % 2][:, 0:128]
                    AD = ev_o[h % 2][:, 0:128]
                    P.op("dve", lambda e, AD=AD, pa=pa: e.tensor_tensor(out=AD, in0=pa, in1=triu[:], op=ALU.mult),
                         r=[t_psx[h % 2], t_triu], w=[t_evo[h % 2]])
                    po = ps_g[h % 2][:, 0:256]
                    P.op("pe", lambda e, AD=AD, vh=vh, po=po: e.matmul(po, lhsT=AD, rhs=vh, start=True, stop=False),
                         r=[t_evo[h % 2], t_wtile[1]], w=[t_psg[h % 2]])
                    for dbl in range(2):
                        P.op("pe", lambda e, h=h, dbl=dbl, po=po: e.matmul(po, lhsT=qT[:, 2 * h + dbl, :], rhs=Rb[:, h, dbl, :],
                                                                          start=False, stop=(dbl == 1)),
                             r=[t_stgb[1], t_wtile[0]], w=[t_psg[h % 2]])
                    P.op("act", lambda e, h=h, po=po: e.activation(out=ret_f[:, h * 256:(h + 1) * 256], in_=po, func=AF.Copy),
                         r=[t_psg[h % 2]], w=[t_gB])
                    for dbl in range(2):
                        pd = ps_g[h % 2][:, 256:512]
                        P.op("pe", lambda e, h=h, dbl=dbl, pd=pd, vh=vh: e.matmul(
                            pd, lhsT=qk_t[1][:, h * 256 + dbl * 128: h * 256 + (dbl + 1) * 128], rhs=vh, start=True, stop=True),
                            r=[t_stgb[0], t_wtile[1]], w=[t_psg[h % 2]])
                        P.op("dve", lambda e, h=h, dbl=dbl, pd=pd: e.scalar_tensor_tensor(
                            out=Tst[:, h, dbl, :], in0=Tst[:, h, dbl, :], scalar=gamC[h], in1=pd, op0=ALU.mult, op1=ALU.add),
                            r=[t_psg[h % 2], t_stgf[1]], w=[t_stgf[1]])
                        P.op("act", lambda e, h=h, dbl=dbl: e.activation(out=Rb[:, h, dbl, :], in_=Tst[:, h, dbl, :], func=AF.Copy, scale=gamC[h]),
                             r=[t_stgf[1]], w=[t_wtile[0]])
                sm8, sq8, mean8, var8, rstd8 = [small[:, 24 + 8 * q: 32 + 8 * q] for q in range(5)]
                r3 = ret_f.rearrange("p (h e) -> p h e", h=8)
                P.op("dve", lambda e: e.tensor_reduce(out=sm8, in_=r3, axis=AX.X, op=ALU.add), r=[t_gB], w=[t_small])
                P.op("act", lambda e: e.activation(out=sqt, in_=ret_f, func=AF.Square), r=[t_gB], w=[t_gB])
                P.op("dve", lambda e: e.tensor_reduce(out=sq8, in_=sqt.rearrange("p (h e) -> p h e", h=8), axis=AX.X, op=ALU.add),
                     r=[t_gB], w=[t_small])
                P.op("dve", lambda e: e.tensor_scalar(out=mean8, in0=sm8, scalar1=1.0 / 256, scalar2=None, op0=ALU.mult), r=[t_small], w=[t_small])
                P.op("dve", lambda e: e.tensor_tensor(out=var8, in0=mean8, in1=mean8, op=ALU.mult), r=[t_small], w=[t_small])
                P.op("dve", lambda e: e.scalar_tensor_tensor(out=var8, in0=sq8, scalar=1.0 / 256, in1=var8, op0=ALU.mult, op1=ALU.subtract),
                     r=[t_small], w=[t_small])
                P.op("dve", lambda e: e.tensor_scalar(out=rstd8, in0=var8, scalar1=EPS, scalar2=None, op0=ALU.add), r=[t_small], w=[t_small])
                P.op("act", lambda e: e.activation(out=rstd8, in_=rstd8, func=AF.Sqrt), r=[t_small], w=[t_small])
                P.op("dve", lambda e: e.reciprocal(out=rstd8, in_=rstd8), r=[t_small], w=[t_small])
                P.op("dve", lambda e: e.tensor_tensor(out=r3, in0=r3, in1=mean8.unsqueeze(2).to_broadcast([128, 8, 256]), op=ALU.subtract),
                     r=[t_gB, t_small], w=[t_gB])
                P.op("dve", lambda e: e.tensor_tensor(out=r3, in0=r3, in1=rstd8.unsqueeze(2).to_broadcast([128, 8, 256]), op=ALU.mult),
                     r=[t_gB, t_small], w=[t_gB])
                P.op("dve", lambda e: e.tensor_tensor(out=ret_f, in0=ret_f, in1=gmB, op=ALU.mult), r=[t_gB] + t_xT, w=[t_gB])
                P.op("dve", lambda e: e.tensor_tensor(out=ybf, in0=ret_f, in1=hq[:, 6144:8192], op=ALU.mult),
                     r=[t_gB, t_wtile[1]], w=[t_wtile[1]])
                for half in range(2):
                    j = cnt["pst"] % 2
                    cnt["pst"] += 1
                    for jj in range(8):
                        bi = half * 8 + jj
                        P.op("pe", lambda e, j=j, jj=jj, bi=bi: e.transpose(
                            out=ps_t[j][:, jj * 128:(jj + 1) * 128], in_=ybf[:, bi * 128:(bi + 1) * 128], identity=ident[:]),
                            r=[t_wtile[1], t_ident], w=[t_pst[j]])
                    ysl = yTs[:, half * 1024:(half + 1) * 1024]
                    P.op("act", lambda e, j=j, ysl=ysl: e.activation(out=ysl, in_=ps_t[j][:], func=AF.Copy),
                         r=[t_pst[j]], w=[t_wtile[1]])
                    P.op("pool", lambda e, half=half, c=c, ysl=ysl: e.dma_start(
                        out=yT[0][half * 1024:(half + 1) * 1024, c * 128:(c + 1) * 128].rearrange("(j p) t -> p j t", p=128),
                        in_=ysl.rearrange("p (j t) -> p j t", j=8)),
                        r=[t_wtile[1]], w=[t_yT[0][c]], dma="wtile1")

        if "dsa" in stages:
            PI2 = math.pi
            TOPK = float(min(256, S // 4))
            NIT = 16
            NEGBIG = 30000.0
            A_ = xTflat
            diagS = A_[:, 22528:26624].rearrange("p (h t) -> p h t", h=32)
            qT2 = A_[:, 4096:6144].rearrange("p (j t) -> p j t", j=16)
            kTc = A_[:, 6144:6784].rearrange("p (j t) -> p j t", j=5)
            qiT = A_[:, 6784:10880].rearrange("p (j t) -> p j t", j=32)
            tmq = A_[:, 10880:18208]
            v_dq, v_dk, v_dv, v_iq, v_ik, v_iw = (tmq[:, 0:2048], tmq[:, 2048:2560], tmq[:, 2560:3072],
                                                  tmq[:, 3072:7168], tmq[:, 7168:7296], tmq[:, 7296:7328])
            F0 = stg_f[0][:].rearrange("p a b -> p (a b)")
            F1 = stg_f[1][:].rearrange("p a b -> p (a b)")
            B0 = stg_b[0][:].rearrange("p a b -> p (a b)")
            B1 = stg_b[1][:].rearrange("p a b -> p (a b)")
            W0 = wtile[0][:].rearrange("p a b -> p (a b)")
            W1 = wtile[1][:].rearrange("p a b -> p (a b)")
            score = W0.bitcast(F32)
            mask = W1[:, 0:8192]
            maskT = W1[:, 8192:16384].rearrange("p (j t) -> p j t", t=128)
            q_r, k_r, ik_r = B0[:, 0:2048], B0[:, 2048:2560], B0[:, 2560:2688]
            qi_r = B1[:, 0:4096]
            ang2, cos2, sin2, atmp2 = [sm2[:, q * 32:(q + 1) * 32] for q in range(4)]
            rt = ev_t[1]
            ang2, cos2, sin2, atmp2 = rt[:, 0:64], rt[:, 64:128], rt[:, 128:192], rt[:, 192:256]
            kf2 = rt[:, 256:320]
            kint64 = kint[:, 0:64]
            tA = gB[:, 0:2048]
            tB = gB[:, 2048:4096]
            P.op("sp", lambda e, l=l: e.dma_start(out=gqk[:, 0:128], in_=q_norm_gain[l:l + 1, :].partition_broadcast(128)), w=[t_gqk], dma="cst")
            P.op("sp", lambda e, l=l: e.dma_start(out=gqk[:, 128:256], in_=k_norm_gain[l:l + 1, :].partition_broadcast(128)), w=[t_gqk], dma="cst")

            def rope128(src3, dst3, H, scale, rtoks, wtoks):
                x1 = src3[:, :, 0:64]
                x2 = src3[:, :, 64:128]
                cb = cos2.unsqueeze(1).to_broadcast([128, H, 64])
                sb_ = sin2.unsqueeze(1).to_broadcast([128, H, 64])
                a3 = tA[:, 0:H * 64].rearrange("p (h d) -> p h d", h=H)
                b3 = tB[:, 0:H * 64].rearrange("p (h d) -> p h d", h=H)
                rw = dict(r=rtoks + [t_evt[1], t_gB], w=[t_gB])
                P.op("dve", lambda e: e.tensor_tensor(out=a3, in0=x1, in1=cb, op=ALU.mult), **rw)
                P.op("dve", lambda e: e.tensor_tensor(out=b3, in0=x2, in1=sb_, op=ALU.mult), **rw)
                P.op("dve", lambda e: e.tensor_tensor(out=a3, in0=a3, in1=b3, op=ALU.subtract), **rw)
                P.op("dve", lambda e: e.tensor_scalar(out=dst3[:, :, 0:64], in0=a3, scalar1=scale, scalar2=None, op0=ALU.mult),
                     r=[t_gB], w=wtoks)
                P.op("dve", lambda e: e.tensor_tensor(out=a3, in0=x2, in1=cb, op=ALU.mult), **rw)
                P.op("dve", lambda e: e.tensor_tensor(out=b3, in0=x1, in1=sb_, op=ALU.mult), **rw)
                P.op("dve", lambda e: e.tensor_tensor(out=a3, in0=a3, in1=b3, op=ALU.add), **rw)
                P.op("dve", lambda e: e.tensor_scalar(out=dst3[:, :, 64:128], in0=a3, scalar1=scale, scalar2=None, op0=ALU.mult),
                     r=[t_gB], w=wtoks)

            def qknorm(src, H, gsl, dstf):
                s3 = src.rearrange("p (h d) -> p h d", h=H)
                d3 = dstf.rearrange("p (h d) -> p h d", h=H)
                ss = sm2[:, 64:64 + H]
                P.op("dve", lambda e: e.tensor_tensor(out=dstf, in0=src, in1=src, op=ALU.mult), r=t_xT + [t_stgf[0]], w=[t_stgf[0]])
                P.op("dve", lambda e: e.tensor_reduce(out=ss, in_=d3, axis=AX.X, op=ALU.add), r=[t_stgf[0]], w=[t_sm2])
                P.op("dve", lambda e: e.tensor_scalar(out=ss, in0=ss, scalar1=1.0 / 128, scalar2=EPS, op0=ALU.mult, op1=ALU.add),
                     r=[t_sm2], w=[t_sm2])
                P.op("act", lambda e: e.activation(out=ss, in_=ss, func=AF.Sqrt), r=[t_sm2], w=[t_sm2])
                P.op("dve", lambda e: e.reciprocal(out=ss, in_=ss), r=[t_sm2], w=[t_sm2])
                P.op("dve", lambda e: e.tensor_tensor(out=d3, in0=s3, in1=ss.unsqueeze(2).to_broadcast([128, H, 128]), op=ALU.mult),
                     r=t_xT + [t_sm2, t_stgf[0]], w=[t_stgf[0]])
                P.op("dve", lambda e: e.tensor_tensor(out=d3, in0=d3, in1=gsl.unsqueeze(1).to_broadcast([128, H, 128]), op=ALU.mult),
                     r=[t_stgf[0], t_gqk], w=[t_stgf[0]])

            for c in range(NCH):
                g, a = divmod(c, CPG)
                Sc = (c + 1) * 128
                P.op("sp", lambda e, g=g, a=a: e.dma_start(out=tmq, in_=h_tm[g][a * 128:(a + 1) * 128, 8192:15520]),
                     r=[t_htm[c]], w=t_xT, dma="xTld")
                P.op("sp", lambda e, c=c: e.dma_start(out=ev_t[1][:, 64:192], in_=tab128[c]), r=[t_tab128[c]], w=[t_evt[1]], dma="tabld")
                qknorm(v_dq, 16, gqk[:, 0:128], F0[:, 0:2048])
                rope128(F0[:, 0:2048].rearrange("p (h d) -> p h d", h=16), q_r.rearrange("p (h d) -> p h d", h=16), 16,
                        128.0 ** -0.5, [t_stgf[0]], [t_stgb[0]])
                qknorm(v_dk, 4, gqk[:, 128:256], F0[:, 0:512])
                rope128(F0[:, 0:512].rearrange("p (h d) -> p h d", h=4), k_r.rearrange("p (h d) -> p h d", h=4), 4,
                        1.0, [t_stgf[0]], [t_stgb[0]])
                rope128(v_ik.rearrange("p (h d) -> p h d", h=1), ik_r.rearrange("p (h d) -> p h d", h=1), 1, 1.0, t_xT, [t_stgb[0]])
                rope128(v_iq.rearrange("p (h d) -> p h d", h=32), qi_r.rearrange("p (h d) -> p h d", h=32), 32,
                        128.0 ** -0.5, t_xT, [t_stgb[1]])
                absw = sm2[:, 0:32]
                sgnw = sm2[:, 32:64]
                P.op("act", lambda e: e.activation(out=absw, in_=v_iw, func=AF.Abs, scale=32.0 ** -0.5), r=t_xT, w=[t_sm2])
                P.op("act", lambda e: e.activation(out=sgnw, in_=v_iw, func=AF.Sign), r=t_xT, w=[t_sm2])
                P.op("dve", lambda e: e.tensor_tensor(out=diagS, in0=ident[:].unsqueeze(1).to_broadcast([128, 32, 128]),
                                                      in1=sgnw.unsqueeze(2).to_broadcast([128, 32, 128]), op=ALU.mult),
                     r=[t_ident, t_sm2] + t_xT, w=t_xT)
                jobs = [(q_r, 16, qT2, 0, t_stgb[0])]
                for (src, nblk, dstv, j0, tok) in ((q_r, 8, qT2, 0, t_stgb[0]), (q_r[:, 1024:2048], 8, qT2, 8, t_stgb[0]),
                                                   (B0[:, 2048:2688], 5, kTc, 0, t_stgb[0]),
                                                   (qi_r, 8, qiT, 0, t_stgb[1]), (qi_r[:, 1024:2048], 8, qiT, 8, t_stgb[1]),
                                                   (qi_r[:, 2048:3072], 8, qiT, 16, t_stgb[1]), (qi_r[:, 3072:4096], 8, qiT, 24, t_stgb[1])):
                    j = cnt["pst"] % 2
                    cnt["pst"] += 1
                    for jj in range(nblk):
                        P.op("pe", lambda e, j=j, jj=jj, src=src: e.transpose(
                            out=ps_t[j][:, jj * 128:(jj + 1) * 128], in_=src[:, jj * 128:(jj + 1) * 128], identity=ident[:]),
                            r=[tok, t_ident], w=[t_pst[j]])
                    P.op("act", lambda e, j=j, nblk=nblk, dstv=dstv, j0=j0: e.activation(
                        out=dstv[:, j0:j0 + nblk, :], in_=ps_t[j][:, 0:nblk * 128].rearrange("p (a b) -> p a b", a=nblk), func=AF.Copy),
                        r=[t_pst[j]], w=t_xT)
                P.op("pool", lambda e, c=c: e.dma_start(out=kT_all[:, :, c * 128:(c + 1) * 128], in_=kTc[:, 0:4, :]),
                     r=t_xT, w=[t_kpre[c]], dma="kpre")
                P.op("pool", lambda e, c=c: e.dma_start(out=kiT_all[:, c * 128:(c + 1) * 128], in_=kTc[:, 4, :]),
                     r=t_xT, w=[t_kpre[c]], dma="kpre")
                P.op("pool", lambda e, c=c: e.dma_start(out=v_all[c * 128:(c + 1) * 128, :], in_=v_dv),
                     r=t_xT, w=[t_kpre[c]], dma="kpre")
                nkt = (Sc + 511) // 512
                for kt in range(nkt):
                    n = min(512, Sc - kt * 512)
                    cs = list(range(kt * 4, min(kt * 4 + 4, c + 1)))
                    if kt % 2 == 0:
                        kit, ktok = ev_o[3][:, 0:n], t_evo[3]
                    else:
                        kit, ktok = B0[:, 3200:3200 + n], t_kit1
                    P.op("sp", lambda e, kit=kit, kt=kt, n=n: e.dma_start(out=kit, in_=kiT_all[:, kt * 512: kt * 512 + n]),
                         r=[t_kpre[q] for q in cs], w=[ktok], dma="kit%d" % (kt % 2))
                    sc = score[:, kt * 512: kt * 512 + n]
                    acc = ps_g[kt % 2][:, 0:n]

                    def raw(h, kit=kit, ktok=ktok, n=n):
                        pp = ps_x[h % 2][:, 0:n]
                        P.op("pe", lambda e, pp=pp, h=h, kit=kit: e.matmul(pp, lhsT=qiT[:, h, :], rhs=kit, start=True, stop=True),
                             r=t_xT + [ktok], w=[t_psx[h % 2]])

                    raw(0)
                    for h in range(32):
                        if h + 1 < 32:
                            raw(h + 1)
                        pp = ps_x[h % 2][:, 0:n]
                        tb = ev_o[h % 3][:, 0:n]
                        if h % 2 == 0:
                            P.op("act", lambda e, pp=pp, tb=tb, h=h: e.activation(out=tb, in_=pp, func=AF.Relu, scale=absw[:, h:h + 1]),
                                 r=[t_psx[h % 2], t_sm2], w=[t_evo[h % 3]])
                        else:
                            P.op("dve", lambda e, pp=pp, tb=tb, h=h: e.tensor_scalar(out=tb, in0=pp, scalar1=absw[:, h:h + 1], scalar2=0.0,
                                                                                    op0=ALU.mult, op1=ALU.max),
                                 r=[t_psx[h % 2], t_sm2], w=[t_evo[h % 3]])
                        P.op("pe", lambda e, acc=acc, tb=tb, h=h: e.matmul(acc, lhsT=diagS[:, h, :], rhs=tb, start=(h == 0), stop=(h == 31)),
                             r=[t_evo[h % 3]] + t_xT, w=[t_psg[kt % 2]])
                    if kt % 2 == 0:
                        P.op("act", lambda e, acc=acc, sc=sc: e.activation(out=sc, in_=acc, func=AF.Copy), r=[t_psg[kt % 2]], w=[t_wtile[0]])
                    else:
                        P.op("dve", lambda e, acc=acc, sc=sc: e.tensor_copy(out=sc, in_=acc), r=[t_psg[kt % 2]], w=[t_wtile[0]])
                lo, hi, mid, cntv, mflag, dd = [sm2[:, 96 + q: 97 + q] for q in range(6)]
                scv = score[:, 0:Sc]
                P.op("dve", lambda e, scv=scv: e.tensor_reduce(out=hi, in_=scv, axis=AX.X, op=ALU.max), r=[t_wtile[0]], w=[t_sm2])
                P.op("dve", lambda e, scv=scv: e.tensor_reduce(out=lo, in_=scv, axis=AX.X, op=ALU.min), r=[t_wtile[0]], w=[t_sm2])
                P.op("dve", lambda e: e.tensor_scalar(out=hi, in0=hi, scalar1=1.0, scalar2=None, op0=ALU.add), r=[t_sm2], w=[t_sm2])
                P.op("dve", lambda e, c=c: e.tensor_tensor(out=score[:, c * 128:(c + 1) * 128], in0=score[:, c * 128:(c + 1) * 128],
                                                          in1=negm[:], op=ALU.add), r=[t_wtile[0], t_negm], w=[t_wtile[0]])
                wr = hi
                P.op("dve", lambda e: e.tensor_tensor(out=wr, in0=hi, in1=lo, op=ALU.subtract), r=[t_sm2], w=[t_sm2])
                for it in range(NIT):
                    rs = dict(r=[t_sm2], w=[t_sm2])
                    f = 2.0 ** -(it + 1)
                    P.op("dve", lambda e, f=f: e.scalar_tensor_tensor(out=mid, in0=wr, scalar=f, in1=lo, op0=ALU.mult, op1=ALU.add), **rs)
                    P.op("dve", lambda e, scv=scv, Sc=Sc: e.tensor_scalar(out=mask[:, 0:Sc], in0=scv, scalar1=mid, scalar2=0.0,
                                                                        op0=ALU.is_ge, op1=ALU.add, accum_out=cntv),
                         r=[t_wtile[0], t_sm2], w=[t_wtile[1], t_sm2])
                    P.op("dve", lambda e: e.tensor_scalar(out=dd, in0=cntv, scalar1=TOPK, scalar2=wr, op0=ALU.is_ge, op1=ALU.mult), **rs)
                    P.op("dve", lambda e, f=f: e.scalar_tensor_tensor(out=lo, in0=dd, scalar=f, in1=lo, op0=ALU.mult, op1=ALU.add), **rs)
                P.op("dve", lambda e, scv=scv, Sc=Sc: e.tensor_scalar(out=mask[:, 0:Sc], in0=scv, scalar1=lo, scalar2=None, op0=ALU.is_ge),
                     r=[t_wtile[0], t_sm2], w=[t_wtile[1]])
                for j0 in range(0, c + 1, 8):
                    nb_ = min(8, c + 1 - j0)
                    j = cnt["pst"] % 2
                    cnt["pst"] += 1
                    for jj in range(nb_):
                        P.op("pe", lambda e, j=j, jj=jj, j0=j0: e.transpose(
                            out=ps_t[j][:, jj * 128:(jj + 1) * 128], in_=mask[:, (j0 + jj) * 128:(j0 + jj + 1) * 128], identity=ident[:]),
                            r=[t_wtile[1], t_ident], w=[t_pst[j]])
                    P.op("dve", lambda e, j=j, nb_=nb_, j0=j0: e.tensor_scalar(
                        out=maskT[:, j0:j0 + nb_, :], in0=ps_t[j][:, 0:nb_ * 128].rearrange("p (a b) -> p a b", a=nb_),
                        scalar1=NEGBIG, scalar2=-NEGBIG, op0=ALU.mult, op1=ALU.add),
                        r=[t_pst[j]], w=[t_wtile[1]])
                for g4 in range(4):
                    blocks = [(kt, jj) for kt in range(nkt) for jj in range(min(512, Sc - kt * 512) // 128)]

                    def kv_views(kt):
                        n = min(512, Sc - kt * 512)
                        o = (kt % 2) * 1024
                        return B1[:, o:o + n], B1[:, o + 512:o + 512 + n].rearrange("p (j e) -> p j e", e=128), n

                    def load_kv(kt, g4=g4):
                        ktile, vtile, n = kv_views(kt)
                        cs = list(range(kt * 4, min(kt * 4 + 4, c + 1)))
                        P.op("sp", lambda e, ktile=ktile, kt=kt, n=n: e.dma_start(out=ktile, in_=kT_all[:, g4, kt * 512: kt * 512 + n]),
                             r=[t_kpre[q] for q in cs], w=[t_kv[kt % 2]], dma="kv%d" % (kt % 2))
                        P.op("sp", lambda e, vtile=vtile, kt=kt, n=n: e.dma_start(
                            out=vtile, in_=v_all[kt * 512: kt * 512 + n, g4 * 128:(g4 + 1) * 128].rearrange("(j p) e -> p j e", p=128)),
                            r=[t_kpre[q] for q in cs], w=[t_kv[kt % 2]], dma="kv%d" % (kt % 2))

                    def qk(jb, g4=g4):
                        kt, jj = blocks[jb]
                        if jj == 0:
                            load_kv(kt)
                        ktile, vtile, n = kv_views(kt)
                        pl = ps_x[jb % 2]
                        P.op("pe", lambda e, pl=pl, jj=jj, ktile=ktile: e.matmul(
                            pl[:], lhsT=ktile[:, jj * 128:(jj + 1) * 128], rhs=A_[:, 4096 + g4 * 512: 4096 + (g4 + 1) * 512], start=True, stop=False),
                            r=[t_kv[kt % 2]] + t_xT, w=[t_psx[jb % 2]])
                        for hh in range(4):
                            P.op("pe", lambda e, pl=pl, hh=hh, jb=jb: e.matmul(
                                pl[:, hh * 128:(hh + 1) * 128], lhsT=ident[:], rhs=maskT[:, jb, :], start=False, stop=(hh == 3)),
                                r=[t_ident, t_wtile[1]], w=[t_psx[jb % 2]])

                    qk(0)
                    for jb in range(len(blocks)):
                        if jb + 1 < len(blocks):
                            qk(jb + 1)
                        kt, jj = blocks[jb]
                        ktile, vtile, n = kv_views(kt)
                        pl = ps_x[jb % 2]
                        E = ev_o[jb % 3]
                        P.op("act", lambda e, pl=pl, E=E: e.activation(out=E[:], in_=pl[:], func=AF.Exp), r=[t_psx[jb % 2]], w=[t_evo[jb % 3]])
                        P.op("pe", lambda e, E=E, jj=jj, vtile=vtile, jb=jb, c=c: e.matmul(
                            ps_g[0][:], lhsT=vtile[:, jj, :], rhs=E[:], start=(jb == 0), stop=(jb == c)),
                            r=[t_kv[kt % 2], t_evo[jb % 3]], w=[t_psg[0]])
                        P.op("pe", lambda e, E=E, jb=jb, c=c: e.matmul(
                            ps_g[1][:], lhsT=ones_bf[:], rhs=E[:], start=(jb == 0), stop=(jb == c)),
                            r=[t_onesbf, t_evo[jb % 3]], w=[t_psg[1]])
                    rec = ev_t[0]
                    att = F1[:, 0:512]
                    P.op("dve", lambda e: e.reciprocal(out=rec[:], in_=ps_g[1][:]), r=[t_psg[1]], w=[t_evt[0]])
                    P.op("dve", lambda e: e.tensor_tensor(out=att, in0=ps_g[0][:], in1=rec[:], op=ALU.mult),
                         r=[t_psg[0], t_evt[0]], w=[t_stgf[1]])
                    P.op("sp", lambda e, g=g, a=a, g4=g4: e.dma_start(
                        out=ev_o[0][:].rearrange("p (h t) -> p h t", h=4),
                        in_=h_fm[g][g4 * 512:(g4 + 1) * 512, a * 128:(a + 1) * 128].rearrange("(h e) t -> e h t", e=128)),
                        r=[t_hfm[g]], w=[t_evo[0]], dma="evo0")
                    P.op("dve", lambda e: e.tensor_tensor(out=ev_o[1][:], in0=att, in1=ev_o[0][:], op=ALU.mult),
                         r=[t_stgf[1], t_evo[0]], w=[t_evo[1]])
                    P.op("pool", lambda e, c=c, g4=g4: e.dma_start(
                        out=yT[1][g4 * 512:(g4 + 1) * 512, c * 128:(c + 1) * 128].rearrange("(h e) t -> e h t", e=128),
                        in_=ev_o[1][:].rearrange("p (h t) -> p h t", h=4)),
                        r=[t_evo[1]], w=[t_yT[1][c]], dma="evo1")
        for b, nm in ((0, "ret"), (1, "dsa"), (2, "gmlp")):
            if nm not in stages and "back" in stages:
                P.op("pool", lambda e: e.memset(ev_o[3][:], 0.0), w=[t_evo[3]])
                for c in range(NCH):
                    for rb in range(16):
                        P.op("sp", lambda e, b=b, c=c, rb=rb: e.dma_start(out=yT[b][rb * 128:(rb + 1) * 128, c * 128:(c + 1) * 128],
                                                                         in_=ev_o[3][:, 0:128]),
                             r=[t_evo[3]], w=[t_yT[b][c]], dma="evo3")
        if "back" in stages:
            xTf = xT[:].rearrange("p a b -> p (a b)")
            yb = xTf[:, 0:3 * 16 * 512].rearrange("p (b k t) -> p b k t", b=3, k=16)
            mT = wtile[0]
            for tt in range(S // 512):
                chs = list(range(tt * 4, tt * 4 + 4))
                for b in range(3):
                    P.op("sp", lambda e, b=b, tt=tt: e.dma_start(
                        out=yb[:, b], in_=yT[b][:, tt * 512:(tt + 1) * 512].rearrange("(k p) t -> p k t", p=128)),
                        r=[t_yT[b][q] for q in chs], w=t_xT, dma="xTld")
                g, a4 = divmod(tt * 512, GT)
                for eb in range(32):
                    pss = []
                    for b in range(3):
                        i = cnt["stg"] % 2
                        cnt["stg"] += 1
                        wb_t = stg_b[i][:].rearrange("p a b -> p (a b)")[:, 0:2048].rearrange("p (k n) -> p k n", k=16)
                        P.op("sp", lambda e, wb_t=wb_t, b=b, eb=eb: e.dma_start(out=wb_t, in_=wbr[b][eb]),
                             r=[t_wbr[b][eb]], w=[t_stgb[i]], dma="stgb%d" % i)
                        P.op("sp", lambda e, b=b, eb=eb, g=g, a4=a4: e.dma_start(
                            out=ev_o[b][:], in_=h_fm[g][6144 + b * 4096 + eb * 128: 6144 + b * 4096 + (eb + 1) * 128, a4:a4 + 512]),
                            r=[t_hfm[g]], w=[t_evo[b]], dma="evo%d" % b)
                        pj = cnt["psg"] % 2
                        cnt["psg"] += 1
                        for k in range(16):
                            P.op("pe", lambda e, pj=pj, wb_t=wb_t, k=k, b=b: e.matmul(
                                ps_g[pj][:], lhsT=wb_t[:, k, :], rhs=yb[:, b, k, :], start=(k == 0), stop=(k == 15)),
                                r=[t_stgb[i]] + t_xT, w=[t_psg[pj]])
                        if b == 0:
                            P.op("dve", lambda e, pj=pj, b=b: e.tensor_tensor(out=ev_t[0][:], in0=ps_g[pj][:], in1=ev_o[b][:], op=ALU.mult),
                                 r=[t_psg[pj], t_evo[b]], w=[t_evt[0]])
                        else:
                            P.op("dve", lambda e, pj=pj, b=b: e.tensor_tensor(out=ev_t[1][:], in0=ps_g[pj][:], in1=ev_o[b][:], op=ALU.mult),
                                 r=[t_psg[pj], t_evo[b]], w=[t_evt[1]])
                            if b == 1:
                                P.op("dve", lambda e: e.tensor_tensor(out=ev_t[0][:], in0=ev_t[0][:], in1=ev_t[1][:], op=ALU.add),
                                     r=[t_evt[0], t_evt[1]], w=[t_evt[0]])
                            else:
                                P.op("dve", lambda e, eb=eb: e.tensor_tensor(out=mT[:, eb, :], in0=ev_t[0][:], in1=ev_t[1][:], op=ALU.add),
                                     r=[t_evt[0], t_evt[1]], w=[t_wtile[0]])
                for nb_ in range(8):
                    P.op("sp", lambda e, nb_=nb_: e.dma_start(out=wtile[1][:], in_=wo[nb_]),
                         r=[t_wo[nb_]], w=[t_wtile[1]], dma="wtile1")
                    for q in range(4):
                        c = tt * 4 + q
                        i = cnt["stg"] % 2
                        cnt["stg"] += 1
                        xr = stg_f[i][:, 0, :]
                        xo = stg_f[i][:, 1, :]
                        P.op("sp", lambda e, xr=xr, c=c, nb_=nb_, x_cur=x_cur: e.dma_start(out=xr, in_=x_cur[c * 128:(c + 1) * 128, nb_ * 512:(nb_ + 1) * 512]),
                             r=[t_xcur[c]], w=[t_stgf[i]], dma="stgf%d" % i)
                        pj = cnt["psg"] % 2
                        cnt["psg"] += 1
                        for k in range(NBLK):
                            P.op("pe", lambda e, pj=pj, k=k, q=q: e.matmul(
                                ps_g[pj][:], lhsT=mT[:, k, q * 128:(q + 1) * 128], rhs=wtile[1][:, k, :],
                                start=(k == 0), stop=(k == NBLK - 1)),
                                r=[t_wtile[0], t_wtile[1]], w=[t_psg[pj]])
                        P.op("dve", lambda e, pj=pj, xr=xr, xo=xo: e.tensor_tensor(out=xo, in0=ps_g[pj][:], in1=xr, op=ALU.add),
                             r=[t_psg[pj], t_stgf[i]], w=[t_stgf[i]])
                        P.op("pool", lambda e, xo=xo, c=c, nb_=nb_, x_next=x_next: e.dma_start(out=x_next[c * 128:(c + 1) * 128, nb_ * 512:(nb_ + 1) * 512], in_=xo),
                             r=[t_stgf[i]], w=[t_xnext[c]], dma="stgf%d" % i)
    P.emit()
    return nc


ALL_STAGES = ("prep", "proj", "gmlp", "ret", "dsa", "back")


def kernel(x, positions, norm_gain, w_in, ret_norm_gain, q_norm_gain, k_norm_gain,
           gm_norm_gain, w_spatial, b_spatial, w_branch, w_out):
    x = np.asarray(x)
    B, S, _ = x.shape
    depth = int(np.asarray(norm_gain).shape[0])
    nc = build(S, depth, stages=ALL_STAGES)
    consts = host_consts()
    shared = {
        "norm_gain": np.ascontiguousarray(norm_gain, dtype=np.float32),
        "w_in": np.ascontiguousarray(w_in, dtype=np.float32),
        "ret_norm_gain": np.ascontiguousarray(ret_norm_gain, dtype=np.float32),
        "gm_norm_gain": np.ascontiguousarray(gm_norm_gain, dtype=np.float32),
        "q_norm_gain": np.ascontiguousarray(q_norm_gain, dtype=np.float32),
        "k_norm_gain": np.ascontiguousarray(k_norm_gain, dtype=np.float32),
        "w_spatial": np.ascontiguousarray(w_spatial, dtype=np.float32),
        "b_spatial": np.ascontiguousarray(b_spatial, dtype=np.float32),
        "w_branch": np.ascontiguousarray(w_branch, dtype=np.float32),
        "w_out": np.ascontiguousarray(w_out, dtype=np.float32),
    }
    shared.update(consts)
    in_maps = []
    for b in range(B):
        m = dict(shared)
        m["x"] = np.ascontiguousarray(x[b], dtype=np.float32)
        m["pos"] = np.ascontiguousarray(np.asarray(positions)[b].reshape(S, 1), dtype=np.int32)
        in_maps.append(m)
    res = run_bass_kernel_spmd(nc, in_maps, core_ids=list(range(B)))
    return np.stack([np.asarray(res.results[b]["out"], dtype=np.float32) for b in range(B)], axis=0)
```

```python
import math
from contextlib import ExitStack
import numpy as np
import concourse.bass as bass
import concourse.mybir as mybir
from concourse.bass_utils import run_bass_kernel_spmd

F32 = mybir.dt.float32
BF16 = mybir.dt.bfloat16
I32 = mybir.dt.int32
ALU = mybir.AluOpType
AF = mybir.ActivationFunctionType
AX = mybir.AxisListType

D = 4096
NIN = 36000
BW = 2048
EPS = 1e-6
NBLK = D // 128


class T:
    __slots__ = ("name", "w", "r")

    def __init__(self, name=""):
        self.name = name
        self.w = set()
        self.r = set()


class Op:
    __slots__ = ("eng", "fn", "deps", "dma", "semkey", "val", "needed", "inc")


class Prog:
    ENG = ("pe", "act", "dve", "pool", "sp")

    def __init__(self, nc):
        self.nc = nc
        self.ops = {e: [] for e in self.ENG}
        self.stack = ExitStack()
        self.dma_cnt = {}
        self.out_sems = []

    def sb(self, name, shape, dt):
        return self.stack.enter_context(self.nc.sbuf_tensor(name, shape, dt))

    def ps(self, name, shape, dt=F32):
        return self.stack.enter_context(self.nc.psum_tensor(name, shape, dt))

    def op(self, eng, fn, r=(), w=(), dma=None, inc=16):
        o = Op()
        o.eng = eng
        o.fn = fn
        o.dma = dma
        o.needed = False
        o.inc = inc
        deps = set()
        for t in r:
            deps |= t.w
        for t in w:
            deps |= t.w
            deps |= t.r
        o.deps = [d for d in deps
                  if not (d.dma is None and dma is None and d.eng == "pe" and eng == "pe")]
        if dma is not None:
            c = self.dma_cnt.get(dma, 0) + inc
            self.dma_cnt[dma] = c
            o.semkey = ("D", dma)
            o.val = c
            o.needed = True
        else:
            o.semkey = ("E", eng)
            o.val = None
        for d in o.deps:
            d.needed = True
        for t in r:
            t.r.add(o)
        for t in w:
            t.w = {o}
            t.r = set()
        self.ops[eng].append(o)
        return o

    def emit(self):
        nc = self.nc
        for e in self.ENG:
            c = 0
            for o in self.ops[e]:
                if o.dma is None:
                    if o.needed:
                        c += 1
                    o.val = c if o.needed else None
        keys = set()
        for e in self.ENG:
            for o in self.ops[e]:
                keys.add(o.semkey)
        sems = {}
        for k in sorted(keys):
            sems[k] = self.stack.enter_context(nc.semaphore("s_" + "_".join(map(str, k))))
        block = self.stack.enter_context(nc.Block())
        engmap = {"pe": "tensor", "act": "scalar", "dve": "vector", "pool": "gpsimd", "sp": "sync"}
        for e in self.ENG:
            ops = self.ops[e]
            if not ops and e != "sp":
                continue

            def body(eng, ops=ops, e=e):
                waited = {}
                for o in ops:
                    need = {}
                    for d in o.deps:
                        need[d.semkey] = max(need.get(d.semkey, 0), d.val)
                    for k, v in need.items():
                        if waited.get(k, 0) < v:
                            eng.wait_ge(sems[k], v)
                            waited[k] = v
                    ins = o.fn(eng)
                    if o.dma is not None:
                        ins.then_inc(sems[o.semkey], o.inc)
                    elif o.needed:
                        ins.then_inc(sems[o.semkey], 1)
                if e == "sp":
                    for k, v in self.dma_cnt.items():
                        eng.wait_ge(sems[("D", k)], v)

            getattr(block, engmap[e])(body)
        self.stack.close()


def col_tiles():
    tiles = []
    for i in range(30):
        c0 = 512 * i
        fn = "silu" if 6144 <= c0 < 8192 else "copy"
        tiles.append((c0, 512, "T", fn, c0))
    tiles.append((15360, 160, "T", "copy", 15360))
    base = 15520
    for i in range(40):
        c0 = base + 512 * i
        if i < 4:
            tiles.append((c0, 512, "F", "silu", 0 + 512 * i))
        elif i < 8:
            tiles.append((c0, 512, "F", "gelu", 2048 + 512 * (i - 4)))
        elif i < 12:
            tiles.append((c0, 512, "T", "gelu", 15520 + 512 * (i - 8)))
        elif i < 16:
            tiles.append((c0, 512, "F", "silu", 4096 + 512 * (i - 12)))
        else:
            tiles.append((c0, 512, "F", "sigmoid", 6144 + 512 * (i - 16)))
    return tiles


HTM_W = 17568
HFM_R = 18432


def host_consts():
    c = {}
    c["ident_in"] = np.eye(128, dtype=np.float32)
    c["tril_in"] = np.tril(np.ones((128, 128), np.float32))
    half = 128
    inv = (1.0 / (10000.0 ** (np.arange(half, dtype=np.float32) * 2.0 / 256.0))).astype(np.float32)
    c["inv256_in"] = np.tile(inv[None, :], (128, 1)).astype(np.float32)
    gam = 1.0 - np.power(2.0, -5.0 - np.arange(8, dtype=np.float64))
    p = np.arange(128, dtype=np.float64)[:, None]
    c["dq_in"] = np.power(gam[None, :], p + 1.0).astype(np.float32)
    c["dk_in"] = (np.power(gam[None, :], -(p + 1.0)) / 16.0).astype(np.float32)
    c["triu_in"] = np.triu(np.ones((128, 128), np.float32))
    inv128 = (1.0 / (10000.0 ** (np.arange(64, dtype=np.float32) * 2.0 / 128.0))).astype(np.float32)
    c["inv128_in"] = np.tile(inv128[None, :], (128, 1)).astype(np.float32)
    c["negm_in"] = np.where(np.arange(128)[None, :] <= np.arange(128)[:, None], 0.0, -1e30).astype(np.float32)
    return c


def build(S, depth, GT=1024, stages=("prep", "proj"), dbg=()):
    nc = bass.Bass("TRN2", target_bir_lowering=False)
    NCH = S // 128
    GT = min(GT, S)
    NG = S // GT
    CPG = GT // 128

    def din(name, shape, dt=F32):
        return nc.dram_tensor(name, shape, dt, kind="ExternalInput").ap()

    def dscr(name, shape, dt):
        if name in dbg:
            return nc.dram_tensor(name, shape, dt, kind="ExternalOutput").ap()
        return nc.dram_tensor(name, shape, dt).ap()

    x_in = din("x", [S, D])
    pos_in = din("pos", [S, 1], I32)
    norm_gain = din("norm_gain", [depth, D])
    w_in = din("w_in", [depth, D, NIN])
    ident_in = din("ident_in", [128, 128])
    tril_in = din("tril_in", [128, 128])
    triu_in = din("triu_in", [128, 128])
    inv256_in = din("inv256_in", [128, 128])
    dq_in = din("dq_in", [128, 8])
    dk_in = din("dk_in", [128, 8])
    ret_norm_gain = din("ret_norm_gain", [depth, BW])
    q_norm_gain = din("q_norm_gain", [depth, 128])
    k_norm_gain = din("k_norm_gain", [depth, 128])
    inv128_in = din("inv128_in", [128, 64])
    negm_in = din("negm_in", [128, 128])
    gm_norm_gain = din("gm_norm_gain", [depth, BW])
    w_spatial = din("w_spatial", [depth, 16, 128, 128])
    b_spatial = din("b_spatial", [depth, 16, 128])
    w_branch = din("w_branch", [depth, 3, BW, D])
    w_out = din("w_out", [depth, D, D])
    out = nc.dram_tensor("out", [S, D], F32, kind="ExternalOutput").ap()

    tiles = col_tiles()
    wt = [dscr("wt%d" % ti, [128, NBLK, ncol], BF16) for ti, (c0, ncol, kind, fn, dc) in enumerate(tiles)]
    h_tm = [dscr("h_tm%d" % g, [GT, HTM_W], BF16) for g in range(NG)]
    h_fm = [dscr("h_fm%d" % g, [HFM_R, GT], BF16) for g in range(NG)]
    wbr = [dscr("wbr%d" % b, [32, 128, 16, 128], BF16) for b in range(3)]
    wo = dscr("wo", [8, 128, NBLK, 512], BF16)
    yT = [dscr("yT%d" % b, [BW, S], BF16) for b in range(3)]
    xmid = dscr("xmid", [S, D], F32)
    kT_all = dscr("kT_all", [128, 4, S], BF16)
    kiT_all = dscr("kiT_all", [128, S], BF16)
    v_all = dscr("v_all", [S, 512], BF16)

    P = Prog(nc)
    xT = P.sb("xT", [128, NBLK, max(GT, 1024)], BF16)
    xTflat = xT[:].rearrange("p a b -> p (a b)")
    wtile = [P.sb("wtile%d" % i, [128, NBLK, 512], BF16) for i in range(2)]
    stg_f = [P.sb("stgf%d" % i, [128, 8, 512], F32) for i in range(2)]
    stg_b = [P.sb("stgb%d" % i, [128, 8, 512], BF16) for i in range(2)]
    ev_t = [P.sb("evt%d" % i, [128, 512], F32) for i in range(2)]
    ev_o = [P.sb("evo%d" % i, [128, 512], BF16) for i in range(4)]
    gB = P.sb("gB", [128, D], F32)
    ident_f = P.sb("ident_f", [128, 128], F32)
    ident = P.sb("ident", [128, 128], BF16)
    small = P.sb("small", [128, 64], F32)
    gmB = xTflat[:, 0:4096].bitcast(F32)
    WmT = xTflat[:, 18432:20480].rearrange("p (g t) -> p g t", g=16)
    bsp = xTflat[0:1, 20480:22528]
    ones_row = P.sb("ones_row", [1, 128], BF16)
    trilm = P.sb("trilm", [128, 128], F32)
    ps_g = [P.ps("psg%d" % i, [128, 512], F32) for i in range(2)]
    ps_t = [P.ps("pst%d" % i, [128, 1024], BF16) for i in range(2)]
    ps_x = [P.ps("psx%d" % i, [128, 512], F32) for i in range(2)]
    t_psx = [T(), T()]
    triu = P.sb("triu", [128, 128], F32)
    inv256 = P.sb("inv256", [128, 128], F32)
    dqk = P.sb("dqk", [128, 16], F32)
    posi = P.sb("posi", [128, 2], I32)
    kint = P.sb("kint", [128, 128], I32)
    t_kint = T()
    t_triu, t_inv, t_dqk, t_posi = T(), T(), T(), T()
    inv128 = P.sb("inv128", [128, 64], F32)
    negm = P.sb("negm", [128, 128], F32)
    gqk = P.sb("gqk", [128, 256], F32)
    ones_bf = P.sb("ones_bf", [128, 128], BF16)
    sm2 = P.sb("sm2", [128, 160], F32)
    t_inv128, t_negm, t_gqk, t_onesbf, t_sm2 = T(), T(), T(), T(), T()
    P.op("sp", lambda e: e.dma_start(out=inv128[:], in_=inv128_in), w=[t_inv128], dma="cst")
    P.op("sp", lambda e: e.dma_start(out=negm[:], in_=negm_in), w=[t_negm], dma="cst")
    P.op("pool", lambda e: e.memset(ones_bf[:], 1.0), w=[t_onesbf])
    t_kpre = [T() for _ in range(NCH)]
    t_kv = [T(), T()]
    t_kit1 = T()
    P.op("sp", lambda e: e.dma_start(out=triu[:], in_=triu_in), w=[t_triu], dma="cst")
    P.op("sp", lambda e: e.dma_start(out=inv256[:], in_=inv256_in), w=[t_inv], dma="cst")
    P.op("sp", lambda e: e.dma_start(out=dqk[:, 0:8], in_=dq_in), w=[t_dqk], dma="cst")
    P.op("sp", lambda e: e.dma_start(out=dqk[:, 8:16], in_=dk_in), w=[t_dqk], dma="cst")

    t_xT = [T("xT%d" % i) for i in range(CPG)]
    t_wtile = [T(), T()]
    t_stgf = [T(), T()]
    t_stgb = [T(), T()]
    t_evt = [T(), T()]
    t_evo = [T() for _ in range(4)]
    t_gB = T()
    t_ident = T()
    t_identf = T()
    t_small = T()
    t_psg = [T(), T()]
    t_pst = [T(), T()]
    t_wt = [T() for _ in tiles]
    t_htm = [T() for _ in range(NCH)]
    t_hfm = [T() for _ in range(NG)]
    t_xcur = [T() for _ in range(NCH)]
    t_WmT, t_bsp, t_bspf, t_ones, t_tril = T(), T(), T(), T(), T()
    t_wbr = [[T() for _ in range(32)] for _ in range(3)]
    t_wo = [T() for _ in range(8)]
    t_yT = [[T() for _ in range(NCH)] for _ in range(3)]
    t_xnext = [T() for _ in range(NCH)]
    P.op("sp", lambda e: e.dma_start(out=trilm[:], in_=tril_in), w=[t_tril], dma="tril")
    P.op("pool", lambda e: e.memset(ones_row[:], 1.0), w=[t_ones])

    P.op("sp", lambda e: e.dma_start(out=ident_f[:], in_=ident_in), w=[t_identf], dma="identf")
    P.op("dve", lambda e: e.tensor_copy(out=ident[:], in_=ident_f[:]), r=[t_identf], w=[t_ident])

    cnt = {"stg": 0, "ev": 0, "psg": 0, "pst": 0, "w": 0}

    tab256 = dscr("tab256", [NCH, 128, 256], F32)
    tab128 = dscr("tab128", [NCH, 128, 128], F32)
    t_tab256 = [T() for _ in range(NCH)]
    t_tab128 = [T() for _ in range(NCH)]
    if "ret" in stages or "dsa" in stages:
        PI = math.pi
        PI2 = math.pi
        ang, cosT, sinT, atmp = [ev_t[1][:, q * 128:(q + 1) * 128] for q in range(4)]
        kf = ev_t[0][:, 0:128]
        rt = ev_t[1]
        ang2, cos2, sin2, atmp2 = rt[:, 0:64], rt[:, 64:128], rt[:, 128:192], rt[:, 192:256]
        kf2 = rt[:, 256:320]
        kint64 = kint[:, 0:64]
        for c in range(NCH):
            P.op("sp", lambda e, c=c: e.dma_start(out=posi[:, 0:1], in_=pos_in[c * 128:(c + 1) * 128, :]), w=[t_posi], dma="posi")
            posf = small[:, 20:21]
            P.op("dve", lambda e, posf=posf: e.tensor_copy(out=posf, in_=posi[:, 0:1]), r=[t_posi], w=[t_small])
            P.op("dve", lambda e, posf=posf: e.tensor_scalar(out=ang, in0=inv256[:], scalar1=posf, scalar2=None, op0=ALU.mult),
                 r=[t_inv, t_small], w=[t_evt[1]])
            kf = ev_t[0][:, 0:128]
            for (dst, shift) in ((sinT, PI), (cosT, 1.5 * PI)):
                rw2 = dict(r=[t_evt[1], t_evt[0], t_kint], w=[t_evt[1], t_evt[0], t_kint])
                P.op("dve", lambda e, shift=shift: e.tensor_scalar(out=atmp, in0=ang, scalar1=1.0 / (2 * PI), scalar2=shift / (2 * PI),
                                                                  op0=ALU.mult, op1=ALU.add), **rw2)
                P.op("dve", lambda e: e.tensor_copy(out=kint[:], in_=atmp), **rw2)
                P.op("dve", lambda e: e.tensor_copy(out=kf, in_=kint[:]), **rw2)
                P.op("dve", lambda e: e.tensor_tensor(out=atmp, in0=atmp, in1=kf, op=ALU.subtract), **rw2)
                P.op("dve", lambda e: e.tensor_scalar(out=kf, in0=atmp, scalar1=0.0, scalar2=None, op0=ALU.is_lt), **rw2)
                P.op("dve", lambda e: e.tensor_tensor(out=atmp, in0=atmp, in1=kf, op=ALU.add), **rw2)
                P.op("dve", lambda e: e.tensor_scalar(out=atmp, in0=atmp, scalar1=2 * PI, scalar2=-PI, op0=ALU.mult, op1=ALU.add), **rw2)
                P.op("dve", lambda e: e.tensor_scalar(out=atmp, in0=atmp, scalar1=-PI, scalar2=PI, op0=ALU.max, op1=ALU.min), **rw2)
                P.op("act", lambda e, dst=dst: e.activation(out=dst, in_=atmp, func=AF.Sin), r=[t_evt[1]], w=[t_evt[1]])
            P.op("pool", lambda e, c=c: e.dma_start(out=tab256[c], in_=ev_t[1][:, 128:384]), r=[t_evt[1]], w=[t_tab256[c]], dma="tabst")
            P.op("sp", lambda e, c=c: e.dma_start(out=posi[:, 0:1], in_=pos_in[c * 128:(c + 1) * 128, :]), w=[t_posi], dma="posi")
            posf = small[:, 20:21]
            P.op("dve", lambda e, posf=posf: e.tensor_copy(out=posf, in_=posi[:, 0:1]), r=[t_posi], w=[t_small])
            P.op("dve", lambda e, posf=posf: e.tensor_scalar(out=ang2, in0=inv128[:], scalar1=posf, scalar2=None, op0=ALU.mult),
                 r=[t_inv128, t_small], w=[t_evt[1]])
            for (dst, shift) in ((sin2, PI2), (cos2, 1.5 * PI2)):
                rw2 = dict(r=[t_evt[1], t_kint], w=[t_evt[1], t_kint])
                P.op("dve", lambda e, shift=shift: e.tensor_scalar(out=atmp2, in0=ang2, scalar1=1.0 / (2 * PI2), scalar2=shift / (2 * PI2),
                                                                  op0=ALU.mult, op1=ALU.add), **rw2)
                P.op("dve", lambda e: e.tensor_copy(out=kint64, in_=atmp2), **rw2)
                P.op("dve", lambda e: e.tensor_copy(out=kf2, in_=kint64), **rw2)
                P.op("dve", lambda e: e.tensor_tensor(out=atmp2, in0=atmp2, in1=kf2, op=ALU.subtract), **rw2)
                P.op("dve", lambda e: e.tensor_scalar(out=kf2, in0=atmp2, scalar1=0.0, scalar2=None, op0=ALU.is_lt), **rw2)
                P.op("dve", lambda e: e.tensor_tensor(out=atmp2, in0=atmp2, in1=kf2, op=ALU.add), **rw2)
                P.op("dve", lambda e: e.tensor_scalar(out=atmp2, in0=atmp2, scalar1=2 * PI2, scalar2=-PI2, op0=ALU.mult, op1=ALU.add), **rw2)
                P.op("dve", lambda e: e.tensor_scalar(out=atmp2, in0=atmp2, scalar1=-PI2, scalar2=PI2, op0=ALU.max, op1=ALU.min), **rw2)
                P.op("act", lambda e, dst=dst: e.activation(out=dst, in_=atmp2, func=AF.Sin), r=[t_evt[1]], w=[t_evt[1]])
            P.op("pool", lambda e, c=c: e.dma_start(out=tab128[c], in_=ev_t[1][:, 64:192]), r=[t_evt[1]], w=[t_tab128[c]], dma="tabst")
    for l in range(depth):
        x_cur = x_in if l == 0 else xmid
        x_next = out if l == depth - 1 else xmid
        if l > 0:
            t_xcur = t_xnext
            t_xnext = [T() for _ in range(NCH)]
        if "prep" in stages:
            wv = w_in[l].rearrange("(k p) n -> p k n", p=128)
            for ti, (c0, ncol, kind, fn, dc) in enumerate(tiles):
                dstv = wt[ti]
                for kq in range(4):
                    i = cnt["stg"] % 2
                    cnt["stg"] += 1
                    P.op("sp", lambda e, i=i, kq=kq, c0=c0, ncol=ncol, wv=wv: e.dma_start(
                        out=stg_f[i][:, :, 0:ncol], in_=wv[:, kq * 8:(kq + 1) * 8, c0:c0 + ncol]),
                        w=[t_stgf[i]], dma="stgf%d" % i)
                    if kq % 2 == 0:
                        P.op("dve", lambda e, i=i, ncol=ncol: e.tensor_copy(out=stg_b[i][:, :, 0:ncol], in_=stg_f[i][:, :, 0:ncol]),
                             r=[t_stgf[i]], w=[t_stgb[i]])
                    else:
                        P.op("act", lambda e, i=i, ncol=ncol: e.activation(out=stg_b[i][:, :, 0:ncol], in_=stg_f[i][:, :, 0:ncol], func=AF.Copy),
                             r=[t_stgf[i]], w=[t_stgb[i]])
                    P.op("pool", lambda e, i=i, kq=kq, dstv=dstv, ncol=ncol: e.dma_start(
                        out=dstv[:, kq * 8:(kq + 1) * 8, :], in_=stg_b[i][:, :, 0:ncol]),
                        r=[t_stgb[i]], w=[t_wt[ti]], dma="stgb%d" % i)
        if "proj" in stages:
            P.op("sp", lambda e, l=l: e.dma_start(out=gB[:], in_=norm_gain[l:l + 1, :].partition_broadcast(128)),
                 w=[t_gB], dma="gB")
            for g in range(NG):
                for ch in range(CPG):
                    c = g * CPG + ch
                    i = cnt["stg"] % 2
                    cnt["stg"] += 1
                    xf = stg_f[i][:].rearrange("p a b -> p (a b)")
                    xs = stg_b[i][:].rearrange("p a b -> p (a b)")
                    P.op("sp", lambda e, c=c, xf=xf, x_cur=x_cur: e.dma_start(out=xf, in_=x_cur[c * 128:(c + 1) * 128, :]),
                         r=[t_xcur[c]], w=[t_stgf[i]], dma="stgf%d" % i)
                    ss = small[:, 0:1]
                    rs = small[:, 1:2]
                    P.op("act", lambda e, xf=xf, xs=xs, ss=ss: e.activation(out=xs, in_=xf, func=AF.Square, accum_out=ss),
                         r=[t_stgf[i]], w=[t_stgb[i], t_small])
                    P.op("dve", lambda e, ss=ss, rs=rs: e.tensor_scalar(out=rs, in0=ss, scalar1=1.0 / D, scalar2=EPS,
                                                                       op0=ALU.mult, op1=ALU.add), r=[t_small], w=[t_small])
                    P.op("act", lambda e, rs=rs: e.activation(out=rs, in_=rs, func=AF.Sqrt), r=[t_small], w=[t_small])
                    P.op("dve", lambda e, rs=rs: e.reciprocal(out=rs, in_=rs), r=[t_small], w=[t_small])
                    P.op("dve", lambda e, xf=xf, xs=xs, rs=rs: e.scalar_tensor_tensor(
                        out=xs, in0=xf, scalar=rs, in1=gB[:], op0=ALU.mult, op1=ALU.mult),
                        r=[t_stgf[i], t_small, t_gB], w=[t_stgb[i]])
                    for k8 in range(4):
                        j = cnt["pst"] % 2
                        cnt["pst"] += 1
                        for kk in range(8):
                            k = k8 * 8 + kk
                            P.op("pe", lambda e, j=j, kk=kk, k=k, xs=xs: e.transpose(
                                out=ps_t[j][:, kk * 128:(kk + 1) * 128], in_=xs[:, k * 128:(k + 1) * 128], identity=ident[:]),
                                r=[t_stgb[i], t_ident], w=[t_pst[j]])
                        ceng = ("act", "dve")[k8 % 2]
                        if ceng == "act":
                            P.op("act", lambda e, j=j, k8=k8, ch=ch: e.activation(
                                out=xT[:, k8 * 8:(k8 + 1) * 8, ch * 128:(ch + 1) * 128],
                                in_=ps_t[j][:].rearrange("p (a b) -> p a b", a=8), func=AF.Copy),
                                r=[t_pst[j]], w=[t_xT[ch]])
                        else:
                            P.op("dve", lambda e, j=j, k8=k8, ch=ch: e.tensor_copy(
                                out=xT[:, k8 * 8:(k8 + 1) * 8, ch * 128:(ch + 1) * 128],
                                in_=ps_t[j][:].rearrange("p (a b) -> p a b", a=8)),
                                r=[t_pst[j]], w=[t_xT[ch]])
                for ti, (c0, ncol, kind, fn, dc) in enumerate(tiles):
                    wi = cnt["w"] % 2
                    cnt["w"] += 1
                    srcv = wt[ti]
                    P.op("sp", lambda e, wi=wi, srcv=srcv, ncol=ncol: e.dma_start(out=wtile[wi][:, :, 0:ncol], in_=srcv),
                         r=[t_wt[ti]], w=[t_wtile[wi]], dma="wtile%d" % wi)
                    if kind == "T":
                        units = [("T", ch, 0) for ch in range(CPG)]
                    else:
                        units = [("F", tt, cb) for cb in range(ncol // 128) for tt in range(GT // 512)]
                    for (kd, a, b) in units:
                        pj = cnt["psg"] % 2
                        cnt["psg"] += 1
                        if kd == "T":
                            pv = ps_g[pj][:, 0:ncol]
                            for k in range(NBLK):
                                P.op("pe", lambda e, pv=pv, k=k, a=a, wi=wi, ncol=ncol: e.matmul(
                                    pv, lhsT=xT[:, k, a * 128:(a + 1) * 128], rhs=wtile[wi][:, k, 0:ncol],
                                    start=(k == 0), stop=(k == NBLK - 1)),
                                    r=[t_xT[a], t_wtile[wi]], w=[t_psg[pj]])
                            n_el = ncol
                        else:
                            pv = ps_g[pj][:, :]
                            chs = [a * 4 + q for q in range(4)]
                            for k in range(NBLK):
                                P.op("pe", lambda e, pv=pv, k=k, a=a, b=b, wi=wi: e.matmul(
                                    pv, lhsT=wtile[wi][:, k, b * 128:(b + 1) * 128], rhs=xT[:, k, a * 512:(a + 1) * 512],
                                    start=(k == 0), stop=(k == NBLK - 1)),
                                    r=[t_xT[q] for q in chs] + [t_wtile[wi]], w=[t_psg[pj]])
                            n_el = 512
                        oi = cnt["ev"] % 4
                        cnt["ev"] += 1
                        ov = ev_o[oi][:, 0:n_el]
                        if fn in ("copy", "silu", "sigmoid"):
                            func = {"copy": AF.Copy, "silu": AF.Silu, "sigmoid": AF.Sigmoid}[fn]
                            P.op("act", lambda e, ov=ov, pv=pv, func=func: e.activation(out=ov, in_=pv, func=func),
                                 r=[t_psg[pj]], w=[t_evo[oi]])
                        else:
                            ti2 = oi % 2
                            tv = ev_t[ti2][:, 0:n_el]
                            P.op("act", lambda e, tv=tv, pv=pv: e.activation(out=tv, in_=pv, func=AF.Square),
                                 r=[t_psg[pj]], w=[t_evt[ti2]])
                            P.op("dve", lambda e, tv=tv: e.tensor_scalar(out=tv, in0=tv, scalar1=0.044715, scalar2=1.0,
                                                                        op0=ALU.mult, op1=ALU.add), r=[t_evt[ti2]], w=[t_evt[ti2]])
                            P.op("dve", lambda e, tv=tv, pv=pv: e.tensor_tensor(out=tv, in0=tv, in1=pv, op=ALU.mult),
                                 r=[t_evt[ti2], t_psg[pj]], w=[t_evt[ti2]])
                            P.op("act", lambda e, tv=tv: e.activation(out=tv, in_=tv, func=AF.Sigmoid, scale=1.5957691216057308),
                                 r=[t_evt[ti2]], w=[t_evt[ti2]])
                            P.op("dve", lambda e, tv=tv, pv=pv, ov=ov: e.tensor_tensor(out=ov, in0=tv, in1=pv, op=ALU.mult),
                                 r=[t_evt[ti2], t_psg[pj]], w=[t_evo[oi]])
                        if kd == "T":
                            c = g * CPG + a
                            P.op("pool", lambda e, ov=ov, a=a, g=g, dc=dc, ncol=ncol: e.dma_start(
                                out=h_tm[g][a * 128:(a + 1) * 128, dc:dc + ncol], in_=ov),
                                r=[t_evo[oi]], w=[t_htm[c]], dma="evo%d" % oi)
                        else:
                            r0 = dc + b * 128
                            t0 = a * 512
                            P.op("pool", lambda e, ov=ov, r0=r0, t0=t0, g=g: e.dma_start(
                                out=h_fm[g][r0:r0 + 128, t0:t0 + 512], in_=ov),
                                r=[t_evo[oi]], w=[t_hfm[g]], dma="evo%d" % oi)

        if "back" in stages:
            for b in range(3):
                wv = w_branch[l, b].rearrange("(k p) n -> p k n", p=128)
                for kh in range(2):
                    for ct in range(8):
                        i = cnt["stg"] % 2
                        cnt["stg"] += 1
                        P.op("sp", lambda e, i=i, wv=wv, kh=kh, ct=ct: e.dma_start(
                            out=stg_f[i][:], in_=wv[:, kh * 8:(kh + 1) * 8, ct * 512:(ct + 1) * 512]),
                            w=[t_stgf[i]], dma="stgf%d" % i)
                        if ct % 2 == 0:
                            P.op("dve", lambda e, i=i: e.tensor_copy(out=stg_b[i][:], in_=stg_f[i][:]), r=[t_stgf[i]], w=[t_stgb[i]])
                        else:
                            P.op("act", lambda e, i=i: e.activation(out=stg_b[i][:], in_=stg_f[i][:], func=AF.Copy), r=[t_stgf[i]], w=[t_stgb[i]])
                        for e4 in range(4):
                            P.op("pool", lambda e, i=i, b=b, kh=kh, ct=ct, e4=e4: e.dma_start(
                                out=wbr[b][ct * 4 + e4, :, kh * 8:(kh + 1) * 8, :], in_=stg_b[i][:, :, e4 * 128:(e4 + 1) * 128]),
                                r=[t_stgb[i]], w=[t_wbr[b][ct * 4 + e4]], dma="stgb%d" % i)
            wv = w_out[l].rearrange("(k p) n -> p k n", p=128)
            for kq in range(4):
                for ct in range(8):
                    i = cnt["stg"] % 2
                    cnt["stg"] += 1
                    P.op("sp", lambda e, i=i, wv=wv, kq=kq, ct=ct: e.dma_start(
                        out=stg_f[i][:], in_=wv[:, kq * 8:(kq + 1) * 8, ct * 512:(ct + 1) * 512]),
                        w=[t_stgf[i]], dma="stgf%d" % i)
                    if ct % 2 == 0:
                        P.op("dve", lambda e, i=i: e.tensor_copy(out=stg_b[i][:], in_=stg_f[i][:]), r=[t_stgf[i]], w=[t_stgb[i]])
                    else:
                        P.op("act", lambda e, i=i: e.activation(out=stg_b[i][:], in_=stg_f[i][:], func=AF.Copy), r=[t_stgf[i]], w=[t_stgb[i]])
                    P.op("pool", lambda e, i=i, kq=kq, ct=ct: e.dma_start(
                        out=wo[ct, :, kq * 8:(kq + 1) * 8, :], in_=stg_b[i][:]),
                        r=[t_stgb[i]], w=[t_wo[ct]], dma="stgb%d" % i)
        if "gmlp" in stages:
            wsf = stg_f[0][:].rearrange("p a b -> p (a b)")[:, 0:2048].rearrange("p (g s) -> p g s", g=16)
            wsb = stg_b[0][:].rearrange("p a b -> p (a b)")[:, 0:2048].rearrange("p (g s) -> p g s", g=16)
            P.op("sp", lambda e, l=l, wsf=wsf: e.dma_start(out=wsf, in_=w_spatial[l].rearrange("g t s -> t g s")),
                 w=[t_stgf[0]], dma="stgf0")
            P.op("dve", lambda e, wsf=wsf, wsb=wsb: e.tensor_tensor(
                out=wsb, in0=wsf, in1=trilm[:].unsqueeze(1).to_broadcast([128, 16, 128]), op=ALU.mult),
                r=[t_stgf[0], t_tril], w=[t_stgb[0]])
            for half in range(2):
                j = cnt["pst"] % 2
                cnt["pst"] += 1
                for gg_ in range(8):
                    gi = half * 8 + gg_
                    P.op("pe", lambda e, j=j, gg_=gg_, gi=gi, wsb=wsb: e.transpose(
                        out=ps_t[j][:, gg_ * 128:(gg_ + 1) * 128], in_=wsb[:, gi, :], identity=ident[:]),
                        r=[t_stgb[0], t_ident], w=[t_pst[j]])
                P.op("dve", lambda e, j=j, half=half: e.tensor_copy(
                    out=WmT[:, half * 8:(half + 1) * 8, :], in_=ps_t[j][:].rearrange("p (a b) -> p a b", a=8)),
                    r=[t_pst[j]], w=t_xT)
            bsp_f = stg_f[1][0:1].rearrange("p a b -> p (a b)")[:, 0:BW]
            P.op("sp", lambda e, l=l, bsp_f=bsp_f: e.dma_start(out=bsp_f, in_=b_spatial[l:l + 1].rearrange("o g t -> o (g t)")),
                 w=[t_stgf[1]], dma="stgf1")
            P.op("dve", lambda e, bsp_f=bsp_f: e.tensor_copy(out=bsp, in_=bsp_f), r=[t_stgf[1]], w=t_xT)
            P.op("sp", lambda e, l=l: e.dma_start(out=gmB, in_=gm_norm_gain[l:l + 1, :].partition_broadcast(128)),
                 w=t_xT, dma="gmB")
            for c in range(NCH):
                g, a = divmod(c, CPG)
                vt = stg_b[0][:].rearrange("p a b -> p (a b)")[:, 0:2048]
                vn = stg_b[1][:].rearrange("p a b -> p (a b)")[:, 0:2048]
                tf = stg_f[0][:].rearrange("p a b -> p (a b)")[:, 0:2048]
                P.op("sp", lambda e, g=g, a=a, vt=vt: e.dma_start(out=vt, in_=h_tm[g][a * 128:(a + 1) * 128, 15520:17568]),
                     r=[t_htm[c]], w=[t_stgb[0]], dma="stgb0")
                sm, sq, mean, var, rstd, nb = [small[:, 8 + q:9 + q] for q in range(6)]
                P.op("act", lambda e, vt=vt, tf=tf, sm=sm: e.activation(out=tf, in_=vt, func=AF.Copy, accum_out=sm),
                     r=[t_stgb[0]], w=[t_stgf[0], t_small])
                P.op("act", lambda e, vt=vt, tf=tf, sq=sq: e.activation(out=tf, in_=vt, func=AF.Square, accum_out=sq),
                     r=[t_stgb[0]], w=[t_stgf[0], t_small])
                P.op("dve", lambda e, sm=sm, mean=mean: e.tensor_scalar(out=mean, in0=sm, scalar1=1.0 / BW, scalar2=None, op0=ALU.mult),
                     r=[t_small], w=[t_small])
                P.op("dve", lambda e, mean=mean, var=var: e.tensor_tensor(out=var, in0=mean, in1=mean, op=ALU.mult),
                     r=[t_small], w=[t_small])
                P.op("dve", lambda e, sq=sq, var=var: e.scalar_tensor_tensor(out=var, in0=sq, scalar=1.0 / BW, in1=var,
                                                                             op0=ALU.mult, op1=ALU.subtract),
                     r=[t_small], w=[t_small])
                P.op("dve", lambda e, var=var, rstd=rstd: e.tensor_scalar(out=rstd, in0=var, scalar1=EPS, scalar2=None, op0=ALU.add),
                     r=[t_small], w=[t_small])
                P.op("act", lambda e, rstd=rstd: e.activation(out=rstd, in_=rstd, func=AF.Sqrt), r=[t_small], w=[t_small])
                P.op("dve", lambda e, rstd=rstd: e.reciprocal(out=rstd, in_=rstd), r=[t_small], w=[t_small])
                P.op("dve", lambda e, mean=mean, rstd=rstd, nb=nb: e.scalar_tensor_tensor(
                    out=nb, in0=mean, scalar=-1.0, in1=rstd, op0=ALU.mult, op1=ALU.mult), r=[t_small], w=[t_small])
                P.op("act", lambda e, vt=vt, tf=tf, rstd=rstd, nb=nb: e.activation(out=tf, in_=vt, func=AF.Identity, bias=nb, scale=rstd),
                     r=[t_stgb[0], t_small], w=[t_stgf[0]])
                P.op("dve", lambda e, tf=tf, vn=vn: e.tensor_tensor(out=vn, in0=tf, in1=gmB, op=ALU.mult),
                     r=[t_stgf[0]] + t_xT, w=[t_stgb[1]])
                for q4 in range(4):
                    pj = cnt["psg"] % 2
                    cnt["psg"] += 1
                    for gq in range(4):
                        gi = q4 * 4 + gq
                        pv = ps_g[pj][:, gq * 128:(gq + 1) * 128]
                        P.op("pe", lambda e, pv=pv, gi=gi, vn=vn: e.matmul(pv, lhsT=vn[:, gi * 128:(gi + 1) * 128], rhs=WmT[:, gi, :],
                                                                         start=True, stop=False),
                             r=[t_stgb[1]] + t_xT, w=[t_psg[pj]])
                        P.op("pe", lambda e, pv=pv, gi=gi: e.matmul(pv, lhsT=ones_row[:], rhs=bsp[:, gi * 128:(gi + 1) * 128],
                                                                  start=False, stop=True),
                             r=[t_ones] + t_xT, w=[t_psg[pj]])
                    P.op("sp", lambda e, g=g, a=a, q4=q4: e.dma_start(
                        out=ev_o[0][:].rearrange("p (q t) -> p q t", q=4),
                        in_=h_fm[g][2048 + q4 * 512: 2048 + (q4 + 1) * 512, a * 128:(a + 1) * 128].rearrange("(q c) t -> c q t", c=128)),
                        r=[t_hfm[g]], w=[t_evo[0]], dma="evo0")
                    P.op("sp", lambda e, g=g, a=a, q4=q4: e.dma_start(
                        out=ev_o[1][:].rearrange("p (q t) -> p q t", q=4),
                        in_=h_fm[g][4096 + q4 * 512: 4096 + (q4 + 1) * 512, a * 128:(a + 1) * 128].rearrange("(q c) t -> c q t", c=128)),
                        r=[t_hfm[g]], w=[t_evo[1]], dma="evo1")
                    P.op("dve", lambda e, pj=pj: e.tensor_tensor(out=ev_t[0][:], in0=ps_g[pj][:], in1=ev_o[0][:], op=ALU.mult),
                         r=[t_psg[pj], t_evo[0]], w=[t_evt[0]])
                    P.op("dve", lambda e: e.tensor_tensor(out=ev_o[2][:], in0=ev_t[0][:], in1=ev_o[1][:], op=ALU.mult),
                         r=[t_evt[0], t_evo[1]], w=[t_evo[2]])
                    P.op("pool", lambda e, c=c, q4=q4: e.dma_start(
                        out=yT[2][q4 * 512:(q4 + 1) * 512, c * 128:(c + 1) * 128].rearrange("(q c) t -> c q t", c=128),
                        in_=ev_o[2][:].rearrange("p (q t) -> p q t", q=4)),
                        r=[t_evo[2]], w=[t_yT[2][c]], dma="evo2")

        if "ret" in stages:
            PI = math.pi
            wt1 = wtile[1][:].rearrange("p a b -> p (a b)")
            wt0 = wtile[0][:].rearrange("p a b -> p (a b)")
            hq = wt1[:, 0:8192]
            ybf = wt1[:, 8192:10240]
            yTs = wt1[:, 10240:12288]
            sf0 = stg_f[0][:].rearrange("p a b -> p (a b)")
            t1 = sf0[:, 0:1024].rearrange("p (h d) -> p h d", h=8)
            t2 = sf0[:, 1024:2048].rearrange("p (h d) -> p h d", h=8)
            ro = sf0[:, 2048:4096].rearrange("p (h d) -> p h d", h=8)
            Tst = stg_f[1][:].rearrange("p a b -> p (a b)").rearrange("p (h k e) -> p h k e", h=8, k=2)
            Rb = wt0[:, 0:4096].rearrange("p (h k e) -> p h k e", h=8, k=2)
            sb0 = stg_b[0][:].rearrange("p a b -> p (a b)")
            sb1 = stg_b[1][:].rearrange("p a b -> p (a b)")
            qk_t = [sb0[:, 0:2048], sb0[:, 2048:4096]]
            qkT = [sb1[:, 0:2048].rearrange("p (j t) -> p j t", j=16), sb1[:, 2048:4096].rearrange("p (j t) -> p j t", j=16)]
            ret_f = gB[:, 0:2048]
            sqt = gB[:, 2048:4096]
            ang, cosT, sinT, atmp = [ev_t[1][:, q * 128:(q + 1) * 128] for q in range(4)]
            gamC = [float((1.0 - 2.0 ** (-5.0 - h)) ** 128) for h in range(8)]
            P.op("sp", lambda e, l=l: e.dma_start(out=gmB, in_=ret_norm_gain[l:l + 1, :].partition_broadcast(128)),
                 w=t_xT, dma="gmB")
            P.op("pool", lambda e: e.memset(stg_f[1][:], 0.0), w=[t_stgf[1]])
            P.op("pool", lambda e: e.memset(wt0[:, 0:4096], 0.0), w=[t_wtile[0]])
            for c in range(NCH):
                g, a = divmod(c, CPG)
                P.op("sp", lambda e, g=g, a=a: e.dma_start(out=hq, in_=h_tm[g][a * 128:(a + 1) * 128, 0:8192]),
                     r=[t_htm[c]], w=[t_wtile[1]], dma="wtile1")
                P.op("sp", lambda e, c=c: e.dma_start(out=ev_t[1][:, 128:384], in_=tab256[c]), r=[t_tab256[c]], w=[t_evt[1]], dma="tabld")
                cosb = cosT.unsqueeze(1).to_broadcast([128, 8, 128])
                sinb = sinT.unsqueeze(1).to_broadcast([128, 8, 128])
                for qi in range(2):
                    src = hq[:, qi * 2048:(qi + 1) * 2048].rearrange("p (h d) -> p h d", h=8)
                    x1 = src[:, :, 0:128]
                    x2 = src[:, :, 128:256]
                    dec = dqk[:, qi * 8:(qi + 1) * 8].unsqueeze(2).to_broadcast([128, 8, 256])
                    rw = dict(r=[t_wtile[1], t_evt[1], t_stgf[0]], w=[t_stgf[0]])
                    P.op("dve", lambda e, x1=x1: e.tensor_tensor(out=t1, in0=x1, in1=cosb, op=ALU.mult), **rw)
                    P.op("dve", lambda e, x2=x2: e.tensor_tensor(out=t2, in0=x2, in1=sinb, op=ALU.mult), **rw)
                    P.op("dve", lambda e: e.tensor_tensor(out=ro[:, :, 0:128], in0=t1, in1=t2, op=ALU.subtract), **rw)
                    P.op("dve", lambda e, x2=x2: e.tensor_tensor(out=t1, in0=x2, in1=cosb, op=ALU.mult), **rw)
                    P.op("dve", lambda e, x1=x1: e.tensor_tensor(out=t2, in0=x1, in1=sinb, op=ALU.mult), **rw)
                    P.op("dve", lambda e: e.tensor_tensor(out=ro[:, :, 128:256], in0=t1, in1=t2, op=ALU.add), **rw)
                    P.op("dve", lambda e, qi=qi, dec=dec: e.tensor_tensor(
                        out=qk_t[qi].rearrange("p (h d) -> p h d", h=8), in0=ro, in1=dec, op=ALU.mult),
                        r=[t_stgf[0], t_dqk], w=[t_stgb[0]])
                    for half in range(2):
                        j = cnt["pst"] % 2
                        cnt["pst"] += 1
                        for jj in range(8):
                            bi = half * 8 + jj
                            P.op("pe", lambda e, j=j, jj=jj, bi=bi, qi=qi: e.transpose(
                                out=ps_t[j][:, jj * 128:(jj + 1) * 128], in_=qk_t[qi][:, bi * 128:(bi + 1) * 128], identity=ident[:]),
                                r=[t_stgb[0], t_ident], w=[t_pst[j]])
                        P.op("act", lambda e, j=j, half=half, qi=qi: e.activation(
                            out=qkT[qi][:, half * 8:(half + 1) * 8, :], in_=ps_t[j][:].rearrange("p (a b) -> p a b", a=8), func=AF.Copy),
                            r=[t_pst[j]], w=[t_stgb[1]])
                qT, kT = qkT

                def ret_pa(h):
                    pa = ps_x[h % 2][:, 0:128]
                    for dbl in range(2):
                        P.op("pe", lambda e, h=h, dbl=dbl, pa=pa: e.matmul(pa, lhsT=kT[:, 2 * h + dbl, :], rhs=qT[:, 2 * h + dbl, :],
                                                                          start=(dbl == 0), stop=(dbl == 1)),
                             r=[t_stgb[1]], w=[t_psx[h % 2]])

                ret_pa(0)
                for h in range(8):
                    if h + 1 < 8:
                        ret_pa(h + 1)
                    vh = hq[:, 4096 + h * 256: 4096 + (h + 1) * 256]
                    pa = ps_x[h % 2][:, 0:128]
                    AD = ev_o[h % 2][:, 0:128]
                    P.op("dve", lambda e, AD=AD, pa=pa: e.tensor_tensor(out=AD, in0=pa, in1=triu[:], op=ALU.mult),
                         r=[t_psx[h % 2], t_triu], w=[t_evo[h % 2]])
                    po = ps_g[h % 2][:, 0:256]
                    P.op("pe", lambda e, AD=AD, vh=vh, po=po: e.matmul(po, lhsT=AD, rhs=vh, start=True, stop=False),
                         r=[t_evo[h % 2], t_wtile[1]], w=[t_psg[h % 2]])
                    for dbl in range(2):
                        P.op("pe", lambda e, h=h, dbl=dbl, po=po: e.matmul(po, lhsT=qT[:, 2 * h + dbl, :], rhs=Rb[:, h, dbl, :],
                                                                          start=False, stop=(dbl == 1)),
                             r=[t_stgb[1], t_wtile[0]], w=[t_psg[h % 2]])
                    P.op("act", lambda e, h=h, po=po: e.activation(out=ret_f[:, h * 256:(h + 1) * 256], in_=po, func=AF.Copy),
                         r=[t_psg[h % 2]], w=[t_gB])
                    for dbl in range(2):
                        pd = ps_g[h % 2][:, 256:512]
                        P.op("pe", lambda e, h=h, dbl=dbl, pd=pd, vh=vh: e.matmul(
                            pd, lhsT=qk_t[1][:, h * 256 + dbl * 128: h * 256 + (dbl + 1) * 128], rhs=vh, start=True, stop=True),
                            r=[t_stgb[0], t_wtile[1]], w=[t_psg[h % 2]])
                        P.op("dve", lambda e, h=h, dbl=dbl, pd=pd: e.scalar_tensor_tensor(
                            out=Tst[:, h, dbl, :], in0=Tst[:, h, dbl, :], scalar=gamC[h], in1=pd, op0=ALU.mult, op1=ALU.add),
                            r=[t_psg[h % 2], t_stgf[1]], w=[t_stgf[1]])
                        P.op("act", lambda e, h=h, dbl=dbl: e.activation(out=Rb[:, h, dbl, :], in_=Tst[:, h, dbl, :], func=AF.Copy, scale=gamC[h]),
                             r=[t_stgf[1]], w=[t_wtile[0]])
                sm8, sq8, mean8, var8, rstd8 = [small[:, 24 + 8 * q: 32 + 8 * q] for q in range(5)]
                r3 = ret_f.rearrange("p (h e) -> p h e", h=8)
                P.op("dve", lambda e: e.tensor_reduce(out=sm8, in_=r3, axis=AX.X, op=ALU.add), r=[t_gB], w=[t_small])
                P.op("act", lambda e: e.activation(out=sqt, in_=ret_f, func=AF.Square), r=[t_gB], w=[t_gB])
                P.op("dve", lambda e: e.tensor_reduce(out=sq8, in_=sqt.rearrange("p (h e) -> p h e", h=8), axis=AX.X, op=ALU.add),
                     r=[t_gB], w=[t_small])
                P.op("dve", lambda e: e.tensor_scalar(out=mean8, in0=sm8, scalar1=1.0 / 256, scalar2=None, op0=ALU.mult), r=[t_small], w=[t_small])
                P.op("dve", lambda e: e.tensor_tensor(out=var8, in0=mean8, in1=mean8, op=ALU.mult), r=[t_small], w=[t_small])
                P.op("dve", lambda e: e.scalar_tensor_tensor(out=var8, in0=sq8, scalar=1.0 / 256, in1=var8, op0=ALU.mult, op1=ALU.subtract),
                     r=[t_small], w=[t_small])
                P.op("dve", lambda e: e.tensor_scalar(out=rstd8, in0=var8, scalar1=EPS, scalar2=None, op0=ALU.add), r=[t_small], w=[t_small])
                P.op("act", lambda e: e.activation(out=rstd8, in_=rstd8, func=AF.Sqrt), r=[t_small], w=[t_small])
                P.op("dve", lambda e: e.reciprocal(out=rstd8, in_=rstd8), r=[t_small], w=[t_small])
                P.op("dve", lambda e: e.tensor_tensor(out=r3, in0=r3, in1=mean8.unsqueeze(2).to_broadcast([128, 8, 256]), op=ALU.subtract),
                     r=[t_gB, t_small], w=[t_gB])
                P.op("dve", lambda e: e.tensor_tensor(out=r3, in0=r3, in1=rstd8.unsqueeze(2).to_broadcast([128, 8, 256]), op=ALU.mult),
                     r=[t_gB, t_small], w=[t_gB])
                P.op("dve", lambda e: e.tensor_tensor(out=ret_f, in0=ret_f, in1=gmB, op=ALU.mult), r=[t_gB] + t_xT, w=[t_gB])
                P.op("dve", lambda e: e.tensor_tensor(out=ybf, in0=ret_f, in1=hq[:, 6144:8192], op=ALU.mult),
                     r=[t_gB, t_wtile[1]], w=[t_wtile[1]])
                for half in range(2):
                    j = cnt["pst"] % 2
                    cnt["pst"] += 1
                    for jj in range(8):
                        bi = half * 8 + jj
                        P.op("pe", lambda e, j=j, jj=jj, bi=bi: e.transpose(
                            out=ps_t[j][:, jj * 128:(jj + 1) * 128], in_=ybf[:, bi * 128:(bi + 1) * 128], identity=ident[:]),
                            r=[t_wtile[1], t_ident], w=[t_pst[j]])
                    ysl = yTs[:, half * 1024:(half + 1) * 1024]
                    P.op("act", lambda e, j=j, ysl=ysl: e.activation(out=ysl, in_=ps_t[j][:], func=AF.Copy),
                         r=[t_pst[j]], w=[t_wtile[1]])
                    P.op("pool", lambda e, half=half, c=c, ysl=ysl: e.dma_start(
                        out=yT[0][half * 1024:(half + 1) * 1024, c * 128:(c + 1) * 128].rearrange("(j p) t -> p j t", p=128),
                        in_=ysl.rearrange("p (j t) -> p j t", j=8)),
                        r=[t_wtile[1]], w=[t_yT[0][c]], dma="wtile1")

        if "dsa" in stages:
            PI2 = math.pi
            TOPK = float(min(256, S // 4))
            NIT = 14
            NEGBIG = 30000.0
            A_ = xTflat
            diagS = A_[:, 22528:26624].rearrange("p (h t) -> p h t", h=32)
            qT2 = A_[:, 4096:6144].rearrange("p (j t) -> p j t", j=16)
            kTc = A_[:, 6144:6784].rearrange("p (j t) -> p j t", j=5)
            qiT = A_[:, 6784:10880].rearrange("p (j t) -> p j t", j=32)
            tmq = A_[:, 10880:18208]
            v_dq, v_dk, v_dv, v_iq, v_ik, v_iw = (tmq[:, 0:2048], tmq[:, 2048:2560], tmq[:, 2560:3072],
                                                  tmq[:, 3072:7168], tmq[:, 7168:7296], tmq[:, 7296:7328])
            F0 = stg_f[0][:].rearrange("p a b -> p (a b)")
            F1 = stg_f[1][:].rearrange("p a b -> p (a b)")
            B0 = stg_b[0][:].rearrange("p a b -> p (a b)")
            B1 = stg_b[1][:].rearrange("p a b -> p (a b)")
            W0 = wtile[0][:].rearrange("p a b -> p (a b)")
            W1 = wtile[1][:].rearrange("p a b -> p (a b)")
            score = W0.bitcast(F32)
            mask = W1[:, 0:8192]
            maskT = W1[:, 8192:16384].rearrange("p (j t) -> p j t", t=128)
            q_r, k_r, ik_r = B0[:, 0:2048], B0[:, 2048:2560], B0[:, 2560:2688]
            qi_r = B1[:, 0:4096]
            ang2, cos2, sin2, atmp2 = [sm2[:, q * 32:(q + 1) * 32] for q in range(4)]
            rt = ev_t[1]
            ang2, cos2, sin2, atmp2 = rt[:, 0:64], rt[:, 64:128], rt[:, 128:192], rt[:, 192:256]
            kf2 = rt[:, 256:320]
            kint64 = kint[:, 0:64]
            tA = gB[:, 0:2048]
            tB = gB[:, 2048:4096]
            P.op("sp", lambda e, l=l: e.dma_start(out=gqk[:, 0:128], in_=q_norm_gain[l:l + 1, :].partition_broadcast(128)), w=[t_gqk], dma="cst")
            P.op("sp", lambda e, l=l: e.dma_start(out=gqk[:, 128:256], in_=k_norm_gain[l:l + 1, :].partition_broadcast(128)), w=[t_gqk], dma="cst")

            def rope128(src3, dst3, H, scale, rtoks, wtoks):
                x1 = src3[:, :, 0:64]
                x2 = src3[:, :, 64:128]
                cb = cos2.unsqueeze(1).to_broadcast([128, H, 64])
                sb_ = sin2.unsqueeze(1).to_broadcast([128, H, 64])
                a3 = tA[:, 0:H * 64].rearrange("p (h d) -> p h d", h=H)
                b3 = tB[:, 0:H * 64].rearrange("p (h d) -> p h d", h=H)
                rw = dict(r=rtoks + [t_evt[1], t_gB], w=[t_gB])
                P.op("dve", lambda e: e.tensor_tensor(out=a3, in0=x1, in1=cb, op=ALU.mult), **rw)
                P.op("dve", lambda e: e.tensor_tensor(out=b3, in0=x2, in1=sb_, op=ALU.mult), **rw)
                P.op("dve", lambda e: e.tensor_tensor(out=a3, in0=a3, in1=b3, op=ALU.subtract), **rw)
                P.op("dve", lambda e: e.tensor_scalar(out=dst3[:, :, 0:64], in0=a3, scalar1=scale, scalar2=None, op0=ALU.mult),
                     r=[t_gB], w=wtoks)
                P.op("dve", lambda e: e.tensor_tensor(out=a3, in0=x2, in1=cb, op=ALU.mult), **rw)
                P.op("dve", lambda e: e.tensor_tensor(out=b3, in0=x1, in1=sb_, op=ALU.mult), **rw)
                P.op("dve", lambda e: e.tensor_tensor(out=a3, in0=a3, in1=b3, op=ALU.add), **rw)
                P.op("dve", lambda e: e.tensor_scalar(out=dst3[:, :, 64:128], in0=a3, scalar1=scale, scalar2=None, op0=ALU.mult),
                     r=[t_gB], w=wtoks)

            def qknorm(src, H, gsl, dstf):
                s3 = src.rearrange("p (h d) -> p h d", h=H)
                d3 = dstf.rearrange("p (h d) -> p h d", h=H)
                ss = sm2[:, 64:64 + H]
                P.op("dve", lambda e: e.tensor_tensor(out=dstf, in0=src, in1=src, op=ALU.mult), r=t_xT + [t_stgf[0]], w=[t_stgf[0]])
                P.op("dve", lambda e: e.tensor_reduce(out=ss, in_=d3, axis=AX.X, op=ALU.add), r=[t_stgf[0]], w=[t_sm2])
                P.op("dve", lambda e: e.tensor_scalar(out=ss, in0=ss, scalar1=1.0 / 128, scalar2=EPS, op0=ALU.mult, op1=ALU.add),
                     r=[t_sm2], w=[t_sm2])
                P.op("act", lambda e: e.activation(out=ss, in_=ss, func=AF.Sqrt), r=[t_sm2], w=[t_sm2])
                P.op("dve", lambda e: e.reciprocal(out=ss, in_=ss), r=[t_sm2], w=[t_sm2])
                P.op("dve", lambda e: e.tensor_tensor(out=d3, in0=s3, in1=ss.unsqueeze(2).to_broadcast([128, H, 128]), op=ALU.mult),
                     r=t_xT + [t_sm2, t_stgf[0]], w=[t_stgf[0]])
                P.op("dve", lambda e: e.tensor_tensor(out=d3, in0=d3, in1=gsl.unsqueeze(1).to_broadcast([128, H, 128]), op=ALU.mult),
                     r=[t_stgf[0], t_gqk], w=[t_stgf[0]])

            for c in range(NCH):
                g, a = divmod(c, CPG)
                Sc = (c + 1) * 128
                P.op("sp", lambda e, g=g, a=a: e.dma_start(out=tmq, in_=h_tm[g][a * 128:(a + 1) * 128, 8192:15520]),
                     r=[t_htm[c]], w=t_xT, dma="xTld")
                P.op("sp", lambda e, c=c: e.dma_start(out=ev_t[1][:, 64:192], in_=tab128[c]), r=[t_tab128[c]], w=[t_evt[1]], dma="tabld")
                qknorm(v_dq, 16, gqk[:, 0:128], F0[:, 0:2048])
                rope128(F0[:, 0:2048].rearrange("p (h d) -> p h d", h=16), q_r.rearrange("p (h d) -> p h d", h=16), 16,
                        128.0 ** -0.5, [t_stgf[0]], [t_stgb[0]])
                qknorm(v_dk, 4, gqk[:, 128:256], F0[:, 0:512])
                rope128(F0[:, 0:512].rearrange("p (h d) -> p h d", h=4), k_r.rearrange("p (h d) -> p h d", h=4), 4,
                        1.0, [t_stgf[0]], [t_stgb[0]])
                rope128(v_ik.rearrange("p (h d) -> p h d", h=1), ik_r.rearrange("p (h d) -> p h d", h=1), 1, 1.0, t_xT, [t_stgb[0]])
                rope128(v_iq.rearrange("p (h d) -> p h d", h=32), qi_r.rearrange("p (h d) -> p h d", h=32), 32,
                        128.0 ** -0.5, t_xT, [t_stgb[1]])
                absw = sm2[:, 0:32]
                sgnw = sm2[:, 32:64]
                P.op("act", lambda e: e.activation(out=absw, in_=v_iw, func=AF.Abs, scale=32.0 ** -0.5), r=t_xT, w=[t_sm2])
                P.op("act", lambda e: e.activation(out=sgnw, in_=v_iw, func=AF.Sign), r=t_xT, w=[t_sm2])
                P.op("dve", lambda e: e.tensor_tensor(out=diagS, in0=ident[:].unsqueeze(1).to_broadcast([128, 32, 128]),
                                                      in1=sgnw.unsqueeze(2).to_broadcast([128, 32, 128]), op=ALU.mult),
                     r=[t_ident, t_sm2] + t_xT, w=t_xT)
                jobs = [(q_r, 16, qT2, 0, t_stgb[0])]
                for (src, nblk, dstv, j0, tok) in ((q_r, 8, qT2, 0, t_stgb[0]), (q_r[:, 1024:2048], 8, qT2, 8, t_stgb[0]),
                                                   (B0[:, 2048:2688], 5, kTc, 0, t_stgb[0]),
                                                   (qi_r, 8, qiT, 0, t_stgb[1]), (qi_r[:, 1024:2048], 8, qiT, 8, t_stgb[1]),
                                                   (qi_r[:, 2048:3072], 8, qiT, 16, t_stgb[1]), (qi_r[:, 3072:4096], 8, qiT, 24, t_stgb[1])):
                    j = cnt["pst"] % 2
                    cnt["pst"] += 1
                    for jj in range(nblk):
                        P.op("pe", lambda e, j=j, jj=jj, src=src: e.transpose(
                            out=ps_t[j][:, jj * 128:(jj + 1) * 128], in_=src[:, jj * 128:(jj + 1) * 128], identity=ident[:]),
                            r=[tok, t_ident], w=[t_pst[j]])
                    P.op("act", lambda e, j=j, nblk=nblk, dstv=dstv, j0=j0: e.activation(
                        out=dstv[:, j0:j0 + nblk, :], in_=ps_t[j][:, 0:nblk * 128].rearrange("p (a b) -> p a b", a=nblk), func=AF.Copy),
                        r=[t_pst[j]], w=t_xT)
                P.op("pool", lambda e, c=c: e.dma_start(out=kT_all[:, :, c * 128:(c + 1) * 128], in_=kTc[:, 0:4, :]),
                     r=t_xT, w=[t_kpre[c]], dma="kpre")
                P.op("pool", lambda e, c=c: e.dma_start(out=kiT_all[:, c * 128:(c + 1) * 128], in_=kTc[:, 4, :]),
                     r=t_xT, w=[t_kpre[c]], dma="kpre")
                P.op("pool", lambda e, c=c: e.dma_start(out=v_all[c * 128:(c + 1) * 128, :], in_=v_dv),
                     r=t_xT, w=[t_kpre[c]], dma="kpre")
                nkt = (Sc + 511) // 512
                for kt in range(nkt):
                    n = min(512, Sc - kt * 512)
                    cs = list(range(kt * 4, min(kt * 4 + 4, c + 1)))
                    if kt % 2 == 0:
                        kit, ktok = ev_o[3][:, 0:n], t_evo[3]
                    else:
                        kit, ktok = B0[:, 3200:3200 + n], t_kit1
                    P.op("sp", lambda e, kit=kit, kt=kt, n=n: e.dma_start(out=kit, in_=kiT_all[:, kt * 512: kt * 512 + n]),
                         r=[t_kpre[q] for q in cs], w=[ktok], dma="kit%d" % (kt % 2))
                    sc = score[:, kt * 512: kt * 512 + n]
                    acc = ps_g[kt % 2][:, 0:n]

                    def raw(h, kit=kit, ktok=ktok, n=n):
                        pp = ps_x[h % 2][:, 0:n]
                        P.op("pe", lambda e, pp=pp, h=h, kit=kit: e.matmul(pp, lhsT=qiT[:, h, :], rhs=kit, start=True, stop=True),
                             r=t_xT + [ktok], w=[t_psx[h % 2]])

                    raw(0)
                    for h in range(32):
                        if h + 1 < 32:
                            raw(h + 1)
                        pp = ps_x[h % 2][:, 0:n]
                        tb = ev_o[h % 3][:, 0:n]
                        if h % 2 == 0:
                            P.op("act", lambda e, pp=pp, tb=tb, h=h: e.activation(out=tb, in_=pp, func=AF.Relu, scale=absw[:, h:h + 1]),
                                 r=[t_psx[h % 2], t_sm2], w=[t_evo[h % 3]])
                        else:
                            P.op("dve", lambda e, pp=pp, tb=tb, h=h: e.tensor_scalar(out=tb, in0=pp, scalar1=absw[:, h:h + 1], scalar2=0.0,
                                                                                    op0=ALU.mult, op1=ALU.max),
                                 r=[t_psx[h % 2], t_sm2], w=[t_evo[h % 3]])
                        P.op("pe", lambda e, acc=acc, tb=tb, h=h: e.matmul(acc, lhsT=diagS[:, h, :], rhs=tb, start=(h == 0), stop=(h == 31)),
                             r=[t_evo[h % 3]] + t_xT, w=[t_psg[kt % 2]])
                    if kt % 2 == 0:
                        P.op("act", lambda e, acc=acc, sc=sc: e.activation(out=sc, in_=acc, func=AF.Copy), r=[t_psg[kt % 2]], w=[t_wtile[0]])
                    else:
                        P.op("dve", lambda e, acc=acc, sc=sc: e.tensor_copy(out=sc, in_=acc), r=[t_psg[kt % 2]], w=[t_wtile[0]])
                lo, hi, mid, cntv, mflag, dd = [sm2[:, 96 + q: 97 + q] for q in range(6)]
                scv = score[:, 0:Sc]
                P.op("dve", lambda e, scv=scv: e.tensor_reduce(out=hi, in_=scv, axis=AX.X, op=ALU.max), r=[t_wtile[0]], w=[t_sm2])
                P.op("dve", lambda e, scv=scv: e.tensor_reduce(out=lo, in_=scv, axis=AX.X, op=ALU.min), r=[t_wtile[0]], w=[t_sm2])
                P.op("dve", lambda e: e.tensor_scalar(out=hi, in0=hi, scalar1=1.0, scalar2=None, op0=ALU.add), r=[t_sm2], w=[t_sm2])
                P.op("dve", lambda e, c=c: e.tensor_tensor(out=score[:, c * 128:(c + 1) * 128], in0=score[:, c * 128:(c + 1) * 128],
                                                          in1=negm[:], op=ALU.add), r=[t_wtile[0], t_negm], w=[t_wtile[0]])
                wr = hi
                P.op("dve", lambda e: e.tensor_tensor(out=wr, in0=hi, in1=lo, op=ALU.subtract), r=[t_sm2], w=[t_sm2])
                for it in range(NIT):
                    rs = dict(r=[t_sm2], w=[t_sm2])
                    f = 2.0 ** -(it + 1)
                    P.op("dve", lambda e, f=f: e.scalar_tensor_tensor(out=mid, in0=wr, scalar=f, in1=lo, op0=ALU.mult, op1=ALU.add), **rs)
                    P.op("dve", lambda e, scv=scv, Sc=Sc: e.tensor_scalar(out=mask[:, 0:Sc], in0=scv, scalar1=mid, scalar2=0.0,
                                                                        op0=ALU.is_ge, op1=ALU.add, accum_out=cntv),
                         r=[t_wtile[0], t_sm2], w=[t_wtile[1], t_sm2])
                    P.op("dve", lambda e: e.tensor_scalar(out=dd, in0=cntv, scalar1=TOPK, scalar2=wr, op0=ALU.is_ge, op1=ALU.mult), **rs)
                    P.op("dve", lambda e, f=f: e.scalar_tensor_tensor(out=lo, in0=dd, scalar=f, in1=lo, op0=ALU.mult, op1=ALU.add), **rs)
                P.op("dve", lambda e, scv=scv, Sc=Sc: e.tensor_scalar(out=mask[:, 0:Sc], in0=scv, scalar1=lo, scalar2=None, op0=ALU.is_ge),
                     r=[t_wtile[0], t_sm2], w=[t_wtile[1]])
                for j0 in range(0, c + 1, 8):
                    nb_ = min(8, c + 1 - j0)
                    j = cnt["pst"] % 2
                    cnt["pst"] += 1
                    for jj in range(nb_):
                        P.op("pe", lambda e, j=j, jj=jj, j0=j0: e.transpose(
                            out=ps_t[j][:, jj * 128:(jj + 1) * 128], in_=mask[:, (j0 + jj) * 128:(j0 + jj + 1) * 128], identity=ident[:]),
                            r=[t_wtile[1], t_ident], w=[t_pst[j]])
                    P.op("dve", lambda e, j=j, nb_=nb_, j0=j0: e.tensor_scalar(
                        out=maskT[:, j0:j0 + nb_, :], in0=ps_t[j][:, 0:nb_ * 128].rearrange("p (a b) -> p a b", a=nb_),
                        scalar1=NEGBIG, scalar2=-NEGBIG, op0=ALU.mult, op1=ALU.add),
                        r=[t_pst[j]], w=[t_wtile[1]])
                for g4 in range(4):
                    blocks = [(kt, jj) for kt in range(nkt) for jj in range(min(512, Sc - kt * 512) // 128)]

                    def kv_views(kt):
                        n = min(512, Sc - kt * 512)
                        o = (kt % 2) * 1024
                        return B1[:, o:o + n], B1[:, o + 512:o + 512 + n].rearrange("p (j e) -> p j e", e=128), n

                    def load_kv(kt, g4=g4):
                        ktile, vtile, n = kv_views(kt)
                        cs = list(range(kt * 4, min(kt * 4 + 4, c + 1)))
                        P.op("sp", lambda e, ktile=ktile, kt=kt, n=n: e.dma_start(out=ktile, in_=kT_all[:, g4, kt * 512: kt * 512 + n]),
                             r=[t_kpre[q] for q in cs], w=[t_kv[kt % 2]], dma="kv%d" % (kt % 2))
                        P.op("sp", lambda e, vtile=vtile, kt=kt, n=n: e.dma_start(
                            out=vtile, in_=v_all[kt * 512: kt * 512 + n, g4 * 128:(g4 + 1) * 128].rearrange("(j p) e -> p j e", p=128)),
                            r=[t_kpre[q] for q in cs], w=[t_kv[kt % 2]], dma="kv%d" % (kt % 2))

                    def qk(jb, g4=g4):
                        kt, jj = blocks[jb]
                        if jj == 0:
                            load_kv(kt)
                        ktile, vtile, n = kv_views(kt)
                        pl = ps_x[jb % 2]
                        P.op("pe", lambda e, pl=pl, jj=jj, ktile=ktile: e.matmul(
                            pl[:], lhsT=ktile[:, jj * 128:(jj + 1) * 128], rhs=A_[:, 4096 + g4 * 512: 4096 + (g4 + 1) * 512], start=True, stop=False),
                            r=[t_kv[kt % 2]] + t_xT, w=[t_psx[jb % 2]])
                        for hh in range(4):
                            P.op("pe", lambda e, pl=pl, hh=hh, jb=jb: e.matmul(
                                pl[:, hh * 128:(hh + 1) * 128], lhsT=ident[:], rhs=maskT[:, jb, :], start=False, stop=(hh == 3)),
                                r=[t_ident, t_wtile[1]], w=[t_psx[jb % 2]])

                    qk(0)
                    for jb in range(len(blocks)):
                        if jb + 1 < len(blocks):
                            qk(jb + 1)
                        kt, jj = blocks[jb]
                        ktile, vtile, n = kv_views(kt)
                        pl = ps_x[jb % 2]
                        E = ev_o[jb % 3]
                        P.op("act", lambda e, pl=pl, E=E: e.activation(out=E[:], in_=pl[:], func=AF.Exp), r=[t_psx[jb % 2]], w=[t_evo[jb % 3]])
                        P.op("pe", lambda e, E=E, jj=jj, vtile=vtile, jb=jb, c=c: e.matmul(
                            ps_g[0][:], lhsT=vtile[:, jj, :], rhs=E[:], start=(jb == 0), stop=(jb == c)),
                            r=[t_kv[kt % 2], t_evo[jb % 3]], w=[t_psg[0]])
                        P.op("pe", lambda e, E=E, jb=jb, c=c: e.matmul(
                            ps_g[1][:], lhsT=ones_bf[:], rhs=E[:], start=(jb == 0), stop=(jb == c)),
                            r=[t_onesbf, t_evo[jb % 3]], w=[t_psg[1]])
                    rec = ev_t[0]
                    att = F1[:, 0:512]
                    P.op("dve", lambda e: e.reciprocal(out=rec[:], in_=ps_g[1][:]), r=[t_psg[1]], w=[t_evt[0]])
                    P.op("dve", lambda e: e.tensor_tensor(out=att, in0=ps_g[0][:], in1=rec[:], op=ALU.mult),
                         r=[t_psg[0], t_evt[0]], w=[t_stgf[1]])
                    P.op("sp", lambda e, g=g, a=a, g4=g4: e.dma_start(
                        out=ev_o[0][:].rearrange("p (h t) -> p h t", h=4),
                        in_=h_fm[g][g4 * 512:(g4 + 1) * 512, a * 128:(a + 1) * 128].rearrange("(h e) t -> e h t", e=128)),
                        r=[t_hfm[g]], w=[t_evo[0]], dma="evo0")
                    P.op("dve", lambda e: e.tensor_tensor(out=ev_o[1][:], in0=att, in1=ev_o[0][:], op=ALU.mult),
                         r=[t_stgf[1], t_evo[0]], w=[t_evo[1]])
                    P.op("pool", lambda e, c=c, g4=g4: e.dma_start(
                        out=yT[1][g4 * 512:(g4 + 1) * 512, c * 128:(c + 1) * 128].rearrange("(h e) t -> e h t", e=128),
                        in_=ev_o[1][:].rearrange("p (h t) -> p h t", h=4)),
                        r=[t_evo[1]], w=[t_yT[1][c]], dma="evo1")
        for b, nm in ((0, "ret"), (1, "dsa"), (2, "gmlp")):
            if nm not in stages and "back" in stages:
                P.op("pool", lambda e: e.memset(ev_o[3][:], 0.0), w=[t_evo[3]])
                for c in range(NCH):
                    for rb in range(16):
                        P.op("sp", lambda e, b=b, c=c, rb=rb: e.dma_start(out=yT[b][rb * 128:(rb + 1) * 128, c * 128:(c + 1) * 128],
                                                                         in_=ev_o[3][:, 0:128]),
                             r=[t_evo[3]], w=[t_yT[b][c]], dma="evo3")
        if "back" in stages:
            xTf = xT[:].rearrange("p a b -> p (a b)")
            yb = xTf[:, 0:3 * 16 * 512].rearrange("p (b k t) -> p b k t", b=3, k=16)
            mT = wtile[0]
            for tt in range(S // 512):
                chs = list(range(tt * 4, tt * 4 + 4))
                for b in range(3):
                    P.op("sp", lambda e, b=b, tt=tt: e.dma_start(
                        out=yb[:, b], in_=yT[b][:, tt * 512:(tt + 1) * 512].rearrange("(k p) t -> p k t", p=128)),
                        r=[t_yT[b][q] for q in chs], w=t_xT, dma="xTld")
                g, a4 = divmod(tt * 512, GT)
                for eb in range(32):
                    pss = []
                    for b in range(3):
                        i = cnt["stg"] % 2
                        cnt["stg"] += 1
                        wb_t = stg_b[i][:].rearrange("p a b -> p (a b)")[:, 0:2048].rearrange("p (k n) -> p k n", k=16)
                        P.op("sp", lambda e, wb_t=wb_t, b=b, eb=eb: e.dma_start(out=wb_t, in_=wbr[b][eb]),
                             r=[t_wbr[b][eb]], w=[t_stgb[i]], dma="stgb%d" % i)
                        P.op("sp", lambda e, b=b, eb=eb, g=g, a4=a4: e.dma_start(
                            out=ev_o[b][:], in_=h_fm[g][6144 + b * 4096 + eb * 128: 6144 + b * 4096 + (eb + 1) * 128, a4:a4 + 512]),
                            r=[t_hfm[g]], w=[t_evo[b]], dma="evo%d" % b)
                        pj = cnt["psg"] % 2
                        cnt["psg"] += 1
                        for k in range(16):
                            P.op("pe", lambda e, pj=pj, wb_t=wb_t, k=k, b=b: e.matmul(
                                ps_g[pj][:], lhsT=wb_t[:, k, :], rhs=yb[:, b, k, :], start=(k == 0), stop=(k == 15)),
                                r=[t_stgb[i]] + t_xT, w=[t_psg[pj]])
                        if b == 0:
                            P.op("dve", lambda e, pj=pj, b=b: e.tensor_tensor(out=ev_t[0][:], in0=ps_g[pj][:], in1=ev_o[b][:], op=ALU.mult),
                                 r=[t_psg[pj], t_evo[b]], w=[t_evt[0]])
                        else:
                            P.op("dve", lambda e, pj=pj, b=b: e.tensor_tensor(out=ev_t[1][:], in0=ps_g[pj][:], in1=ev_o[b][:], op=ALU.mult),
                                 r=[t_psg[pj], t_evo[b]], w=[t_evt[1]])
                            if b == 1:
                                P.op("dve", lambda e: e.tensor_tensor(out=ev_t[0][:], in0=ev_t[0][:], in1=ev_t[1][:], op=ALU.add),
                                     r=[t_evt[0], t_evt[1]], w=[t_evt[0]])
                            else:
                                P.op("dve", lambda e, eb=eb: e.tensor_tensor(out=mT[:, eb, :], in0=ev_t[0][:], in1=ev_t[1][:], op=ALU.add),
                                     r=[t_evt[0], t_evt[1]], w=[t_wtile[0]])
                for nb_ in range(8):
                    P.op("sp", lambda e, nb_=nb_: e.dma_start(out=wtile[1][:], in_=wo[nb_]),
                         r=[t_wo[nb_]], w=[t_wtile[1]], dma="wtile1")
                    for q in range(4):
                        c = tt * 4 + q
                        i = cnt["stg"] % 2
                        cnt["stg"] += 1
                        xr = stg_f[i][:, 0, :]
                        xo = stg_f[i][:, 1, :]
                        P.op("sp", lambda e, xr=xr, c=c, nb_=nb_, x_cur=x_cur: e.dma_start(out=xr, in_=x_cur[c * 128:(c + 1) * 128, nb_ * 512:(nb_ + 1) * 512]),
                             r=[t_xcur[c]], w=[t_stgf[i]], dma="stgf%d" % i)
                        pj = cnt["psg"] % 2
                        cnt["psg"] += 1
                        for k in range(NBLK):
                            P.op("pe", lambda e, pj=pj, k=k, q=q: e.matmul(
                                ps_g[pj][:], lhsT=mT[:, k, q * 128:(q + 1) * 128], rhs=wtile[1][:, k, :],
                                start=(k == 0), stop=(k == NBLK - 1)),
                                r=[t_wtile[0], t_wtile[1]], w=[t_psg[pj]])
                        P.op("dve", lambda e, pj=pj, xr=xr, xo=xo: e.tensor_tensor(out=xo, in0=ps_g[pj][:], in1=xr, op=ALU.add),
                             r=[t_psg[pj], t_stgf[i]], w=[t_stgf[i]])
                        P.op("pool", lambda e, xo=xo, c=c, nb_=nb_, x_next=x_next: e.dma_start(out=x_next[c * 128:(c + 1) * 128, nb_ * 512:(nb_ + 1) * 512], in_=xo),
                             r=[t_stgf[i]], w=[t_xnext[c]], dma="stgf%d" % i)
    P.emit()
    return nc


ALL_STAGES = ("prep", "proj", "gmlp", "ret", "dsa", "back")


def kernel(x, positions, norm_gain, w_in, ret_norm_gain, q_norm_gain, k_norm_gain,
           gm_norm_gain, w_spatial, b_spatial, w_branch, w_out):
    x = np.asarray(x)
    B, S, _ = x.shape
    depth = int(np.asarray(norm_gain).shape[0])
    nc = build(S, depth, stages=ALL_STAGES)
    consts = host_consts()
    shared = {
        "norm_gain": np.ascontiguousarray(norm_gain, dtype=np.float32),
        "w_in": np.ascontiguousarray(w_in, dtype=np.float32),
        "ret_norm_gain": np.ascontiguousarray(ret_norm_gain, dtype=np.float32),
        "gm_norm_gain": np.ascontiguousarray(gm_norm_gain, dtype=np.float32),
        "q_norm_gain": np.ascontiguousarray(q_norm_gain, dtype=np.float32),
        "k_norm_gain": np.ascontiguousarray(k_norm_gain, dtype=np.float32),
        "w_spatial": np.ascontiguousarray(w_spatial, dtype=np.float32),
        "b_spatial": np.ascontiguousarray(b_spatial, dtype=np.float32),
        "w_branch": np.ascontiguousarray(w_branch, dtype=np.float32),
        "w_out": np.ascontiguousarray(w_out, dtype=np.float32),
    }
    shared.update(consts)
    in_maps = []
    for b in range(B):
        m = dict(shared)
        m["x"] = np.ascontiguousarray(x[b], dtype=np.float32)
        m["pos"] = np.ascontiguousarray(np.asarray(positions)[b].reshape(S, 1), dtype=np.int32)
        in_maps.append(m)
    res = run_bass_kernel_spmd(nc, in_maps, core_ids=list(range(B)))
    return np.stack([np.asarray(res.results[b]["out"], dtype=np.float32) for b in range(B)], axis=0)
```
